# Optimizing a Trainium2 kernel written in Bass

```python
import math
import jax, jax.numpy as jnp
from jax import lax
import numpy as np

D_MODEL = 1024
BATCH = 8
SEQ = 4096
DEPTH = 1

HEAD_DIM = 64
DILATIONS = (1, 4, 16)
KEYS_PER_QUERY = 128
WINDOWS = tuple(KEYS_PER_QUERY * d for d in DILATIONS)
N_GROUPS = len(DILATIONS)
HEADS_PER_GROUP = 4
ATT_HEADS = N_GROUPS * HEADS_PER_GROUP
ATT_WIDTH = ATT_HEADS * HEAD_DIM
ATT_OUT_WIDTH = HEADS_PER_GROUP * HEAD_DIM
QUERY_BLOCK = 64
N_BUCKETS = 32
MAX_DISTANCE = WINDOWS[-1]
RWKV_HEADS = D_MODEL // HEAD_DIM
RWKV_WIDTH = RWKV_HEADS * HEAD_DIM
DECAY_LORA = 64
ICLR_LORA = 64
GATE_LORA = 128
RWKV_COLS = 3 * RWKV_WIDTH + DECAY_LORA + ICLR_LORA + GATE_LORA
N_BRANCHES = 2
IN_COLS = 3 * ATT_WIDTH + RWKV_COLS + N_BRANCHES * D_MODEL
D_FF = 4 * D_MODEL
NORM_EPS = 1e-6
LN_X_EPS = 64e-5
L2_EPS = 1e-12

kernel_name = "hybrid_dilated_attn_rwkv7_gated_block"


def rmsnorm(x, g):
    xf = x.astype(jnp.float32)
    y = xf * lax.rsqrt(jnp.mean(xf * xf, axis=-1, keepdims=True) + NORM_EPS)
    return (y * g.astype(jnp.float32)).astype(x.dtype)


def t5_bucket(dist):
    max_exact = N_BUCKETS // 2
    d_f = jnp.maximum(dist, 1).astype(jnp.float32)
    large = max_exact + (jnp.log(d_f / max_exact) / math.log(MAX_DISTANCE / max_exact)
                         * (N_BUCKETS - max_exact)).astype(jnp.int32)
    large = jnp.minimum(large, N_BUCKETS - 1)
    return jnp.where(dist < max_exact, dist, large)


def dilated_window_attention(q, k, v, rel_bias):
    b, s = q.shape[0], q.shape[1]
    dil = jnp.array(DILATIONS, jnp.int32)
    dist = dil[:, None] * jnp.arange(KEYS_PER_QUERY + 1, dtype=jnp.int32)[None, :]
    bucket = t5_bucket(dist)
    bias = rel_bias.reshape(N_BUCKETS, N_GROUPS, HEADS_PER_GROUP)[bucket, jnp.arange(N_GROUPS)[:, None]]
    bias = jnp.transpose(bias, (0, 2, 1)).astype(jnp.float32)
    qg = jnp.moveaxis(q, 2, 0)
    kg = jnp.moveaxis(k, 2, 0)
    vg = jnp.moveaxis(v, 2, 0)
    scale = HEAD_DIM ** -0.5

    def one_block(blk):
        start = blk * QUERY_BLOCK
        t = start + jnp.arange(QUERY_BLOCK, dtype=jnp.int32)
        pos = t[None, :, None] - dist[:, None, :]
        valid = pos >= 0
        pos = jnp.maximum(pos, 0)
        qb = lax.dynamic_slice_in_dim(qg, start, QUERY_BLOCK, axis=2)
        kb = jax.vmap(lambda a, p: a[:, p])(kg, pos)
        vb = jax.vmap(lambda a, p: a[:, p])(vg, pos)
        logits = jnp.einsum('gbqhd,gbqjhd->gbhqj', qb, kb,
                            preferred_element_type=jnp.float32) * scale
        logits = logits + bias[:, None, :, None, :]
        logits = jnp.where(valid[:, None, None, :, :], logits, -jnp.inf)
        lse = jax.nn.logsumexp(logits, axis=-1)
        p = jnp.exp(logits - lse[..., None])
        o = jnp.einsum('gbhqj,gbqjhd->gbqhd', p, vb.astype(jnp.float32))
        w_mix = jax.nn.softmax(lse, axis=0)
        o = jnp.einsum('gbhq,gbqhd->bqhd', w_mix, o)
        return o.astype(v.dtype)

    out = lax.map(one_block, jnp.arange(s // QUERY_BLOCK, dtype=jnp.int32))
    return jnp.moveaxis(out, 0, 1).reshape(b, s, ATT_OUT_WIDTH)


def token_shift(f, mu):
    f_prev = jnp.pad(f, ((0, 0), (1, 0), (0, 0)))[:, :-1]
    return f + (f_prev - f) * mu


def rwkv7_time_mix(f, w0, w_w2, a0, w_a2, w_g2, k_k, k_a, r_k, ln_w, ln_b):
    b, s, _ = f.shape
    splits = [RWKV_WIDTH, 2 * RWKV_WIDTH, 3 * RWKV_WIDTH,
              3 * RWKV_WIDTH + DECAY_LORA, 3 * RWKV_WIDTH + DECAY_LORA + ICLR_LORA]
    r, k, v, fw, fa, fg = jnp.split(f, splits, axis=-1)
    w = -jax.nn.softplus(-(w0 + jnp.tanh(fw) @ w_w2)) - 0.5
    decay = jnp.exp(-jnp.exp(w.astype(jnp.float32)))
    a = jax.nn.sigmoid(a0 + fa @ w_a2)
    g = jax.nn.sigmoid(fg) @ w_g2
    kk = (k * k_k).reshape(b, s, RWKV_HEADS, HEAD_DIM).astype(jnp.float32)
    kk = kk / jnp.maximum(jnp.linalg.norm(kk, axis=-1, keepdims=True), L2_EPS)
    k = k * (1 + (a - 1) * k_a)
    heads = lambda t: t.reshape(b, s, RWKV_HEADS, HEAD_DIM).astype(jnp.float32)
    r, k, v, decay, a = heads(r), heads(k), heads(v), heads(decay), heads(a)

    def step(state, inp):
        r_t, w_t, k_t, v_t, kk_t, a_t = inp
        sa = jnp.einsum('bhij,bhj->bhi', state, -kk_t)
        state = (state * w_t[:, :, None, :]
                 + sa[..., None] * (kk_t * a_t)[:, :, None, :]
                 + v_t[..., None] * k_t[:, :, None, :])
        y_t = jnp.einsum('bhij,bhj->bhi', state, r_t)
        return state, y_t

    xs = tuple(jnp.moveaxis(t, 1, 0) for t in (r, decay, k, v, kk, a))
    state0 = jnp.zeros((b, RWKV_HEADS, HEAD_DIM, HEAD_DIM), jnp.float32)
    _, y = lax.scan(step, state0, xs)
    y = jnp.moveaxis(y, 0, 1)
    mu = jnp.mean(y, axis=-1, keepdims=True)
    var = jnp.mean(jnp.square(y - mu), axis=-1, keepdims=True)
    y = ((y - mu) * lax.rsqrt(var + LN_X_EPS)).reshape(b, s, RWKV_WIDTH)
    y = y * ln_w.astype(jnp.float32) + ln_b.astype(jnp.float32)
    bonus = jnp.sum(r * k * r_k.astype(jnp.float32), axis=-1, keepdims=True) * v
    y = (y + bonus.reshape(b, s, RWKV_WIDTH)) * g.astype(jnp.float32)
    return y.astype(f.dtype)


def setup_inputs(seed: int = 0) -> dict:
    key = jax.random.key(seed)
    ks = jax.random.split(key, 32)
    L = DEPTH
    nrm = lambda k, shape, scale: jax.random.normal(k, shape, jnp.float32) * scale
    gain = lambda k, shape: 1.0 + 0.05 * jax.random.normal(k, shape, jnp.float32)
    return {
        "x": nrm(ks[0], (BATCH, SEQ, D_MODEL), 1.0),
        "rel_bias": nrm(ks[1], (N_BUCKETS, ATT_HEADS), 0.2),
        "norm_mix_pre": gain(ks[2], (L, D_MODEL)),
        "norm_mix_post": gain(ks[3], (L, D_MODEL)),
        "norm_ffn_pre": gain(ks[4], (L, D_MODEL)),
        "norm_ffn_post": gain(ks[5], (L, D_MODEL)),
        "w_in": nrm(ks[6], (L, D_MODEL, IN_COLS), D_MODEL ** -0.5),
        "b_gate": nrm(ks[7], (L, N_BRANCHES * D_MODEL), 0.02),
        "shift_mu": jax.random.uniform(ks[8], (L, RWKV_COLS), jnp.float32),
        "w0": jax.random.uniform(ks[9], (L, RWKV_WIDTH), jnp.float32, -6.0, 1.0),
        "w_w2": nrm(ks[10], (L, DECAY_LORA, RWKV_WIDTH), 0.1 * DECAY_LORA ** -0.5),
        "a0": nrm(ks[11], (L, RWKV_WIDTH), 0.1),
        "w_a2": nrm(ks[12], (L, ICLR_LORA, RWKV_WIDTH), 0.5 * ICLR_LORA ** -0.5),
        "w_g2": nrm(ks[13], (L, GATE_LORA, RWKV_WIDTH), GATE_LORA ** -0.5),
        "k_k": 0.85 + 0.05 * jax.random.normal(ks[14], (L, RWKV_WIDTH), jnp.float32),
        "k_a": gain(ks[15], (L, RWKV_WIDTH)),
        "r_k": nrm(ks[16], (L, RWKV_HEADS, HEAD_DIM), 0.1),
        "ln_x_w": gain(ks[17], (L, RWKV_WIDTH)),
        "ln_x_b": nrm(ks[18], (L, RWKV_WIDTH), 0.02),
        "w_att_branch": nrm(ks[19], (L, ATT_OUT_WIDTH, D_MODEL), ATT_OUT_WIDTH ** -0.5),
        "w_rwkv_branch": nrm(ks[20], (L, RWKV_WIDTH, D_MODEL), RWKV_WIDTH ** -0.5),
        "w_out": nrm(ks[21], (L, D_MODEL, D_MODEL), D_MODEL ** -0.5),
        "w_ffn1": nrm(ks[22], (L, D_MODEL, D_FF), D_MODEL ** -0.5),
        "w_ffn2": nrm(ks[23], (L, D_FF, D_MODEL), D_FF ** -0.5),
    }


def reference(x, rel_bias, norm_mix_pre, norm_mix_post, norm_ffn_pre, norm_ffn_post,
              w_in, b_gate, shift_mu, w0, w_w2, a0, w_a2, w_g2, k_k, k_a, r_k,
              ln_x_w, ln_x_b, w_att_branch, w_rwkv_branch, w_out, w_ffn1, w_ffn2):
    b, s, _ = x.shape
    for l in range(DEPTH):
        h = rmsnorm(x, norm_mix_pre[l])
        proj = h @ w_in[l]
        f_att, f_rwkv, f_gate = jnp.split(
            proj, [3 * ATT_WIDTH, 3 * ATT_WIDTH + RWKV_COLS], axis=-1)
        qkv = f_att.reshape(b, s, 3, N_GROUPS, HEADS_PER_GROUP, HEAD_DIM)
        o_att = dilated_window_attention(qkv[:, :, 0], qkv[:, :, 1], qkv[:, :, 2], rel_bias)
        o_rwkv = rwkv7_time_mix(token_shift(f_rwkv, shift_mu[l]), w0[l], w_w2[l], a0[l],
                                w_a2[l], w_g2[l], k_k[l], k_a[l], r_k[l],
                                ln_x_w[l], ln_x_b[l])
        gates = jax.nn.sigmoid(f_gate + b_gate[l]).reshape(b, s, N_BRANCHES, D_MODEL)
        merged = (gates[:, :, 0] * (o_att @ w_att_branch[l])
                  + gates[:, :, 1] * (o_rwkv @ w_rwkv_branch[l]))
        x = x + rmsnorm(merged @ w_out[l], norm_mix_post[l])
        h = rmsnorm(x, norm_ffn_pre[l])
        ff = jnp.square(jax.nn.relu(h @ w_ffn1[l])) @ w_ffn2[l]
        x = x + rmsnorm(ff, norm_ffn_post[l])
    return x
```

```python
import contextlib
import numpy as np
import concourse.bass as bass
import concourse.mybir as mybir
from concourse.bass_utils import run_bass_kernel_spmd

F32 = mybir.dt.float32
BF16 = mybir.dt.bfloat16
U8 = mybir.dt.uint8
AF = mybir.ActivationFunctionType
ALU = mybir.AluOpType
AX = mybir.AxisListType

S_LEN = 4096
D = 1024
NEG = -30000.0


class Res:
    __slots__ = ("name", "w", "r")

    def __init__(self, name=""):
        self.name = name
        self.w = None
        self.r = {}


class Sched:
    COMPUTE = ("pe", "act", "dve", "pool")
    QUEUES = ("pe", "act", "dve", "pool", "sp")

    def __init__(self, nc, n_dma_sems=48, same_engine_sync=True):
        self.nc = nc
        self.ops = {e: [] for e in self.QUEUES}
        self.cnt = {e: 0 for e in self.COMPUTE}
        self.known = {e: {} for e in self.QUEUES}
        self.n_dma = n_dma_sems
        self.dval = [0] * n_dma_sems
        self.dnext = 0
        self.dq = {}
        self.same = same_engine_sync
        self.sems = {}
        self.nops = 0

    def _add_wait(self, eng, waits, tok):
        if tok is None:
            return
        sk, v = tok
        if sk == eng and (eng == "pe" or not self.same):
            return
        if self.known[eng].get(sk, 0) >= v:
            return
        self.known[eng][sk] = v
        waits[sk] = max(waits.get(sk, 0), v)

    def op(self, eng, fn, reads=(), writes=(), dma=False):
        waits = {}
        for r in reads:
            self._add_wait(eng, waits, r.w)
        for w in writes:
            self._add_wait(eng, waits, w.w)
            for sk, v in w.r.items():
                self._add_wait(eng, waits, (sk, v))
        if dma:
            half = self.n_dma // 2
            base = 0 if eng == "sp" else half
            self.dq[eng] = (self.dq.get(eng, -1) + 1) % half
            k = base + self.dq[eng]
            sk = ("d", k)
            if self.dval[k] > 0:
                self._add_wait(eng, waits, (sk, self.dval[k]))
            self.dval[k] += 16
            tok = (sk, self.dval[k])
            inc = (sk, 16)
        else:
            self.cnt[eng] += 1
            tok = (eng, self.cnt[eng])
            inc = (eng, 1)
        self.ops[eng].append((list(waits.items()), fn, inc))
        self.nops += 1
        for r in reads:
            sk, v = tok
            if r.r.get(sk, 0) < v:
                r.r[sk] = v
        for w in writes:
            w.w = tok
            w.r = {}
        return tok

    def barrier(self):
        if getattr(self, "pe_reset", None) is not None:
            self.pe_reset()
        toks = [(("d", k), self.dval[k]) for k in range(self.n_dma) if self.dval[k] > 0]
        toks += [(e, self.cnt[e]) for e in self.COMPUTE if self.cnt[e] > 0]
        for eng in self.QUEUES:
            waits = {}
            for sk, v in toks:
                if self.known[eng].get(sk, 0) >= v:
                    continue
                self.known[eng][sk] = v
                waits[sk] = v
            if waits:
                self.ops[eng].append((list(waits.items()), None, None))

    def emit(self):
        nc = self.nc
        with contextlib.ExitStack() as st:
            keys = list(self.COMPUTE) + [("d", k) for k in range(self.n_dma)]
            for k in keys:
                nm = k if isinstance(k, str) else "d%d" % k[1]
                self.sems[k] = st.enter_context(nc.semaphore("s_" + nm))
            block = st.enter_context(nc.Block())
            sems = self.sems

            import os as _os2
            attach = int(_os2.environ.get("DBG_ATTACH", 1))

            def replay(name, eng):
                for waits, fn, inc in self.ops[name]:
                    if fn is None or not attach or not waits:
                        for sk, v in waits:
                            eng.wait_ge(sems[sk], v)
                        if fn is None:
                            continue
                        ins = fn(eng)
                    else:
                        for sk, v in waits[:-1]:
                            eng.wait_ge(sems[sk], v)
                        ins = fn(eng)
                        ins._wait_ge(sems[waits[-1][0]], waits[-1][1])
                    ins.then_inc(sems[inc[0]], inc[1])

            @block.tensor
            def _(eng):
                replay("pe", eng)

            @block.scalar
            def _(eng):
                replay("act", eng)

            @block.vector
            def _(eng):
                replay("dve", eng)

            @block.gpsimd
            def _(eng):
                replay("pool", eng)

            @block.sync
            def _(eng):
                replay("sp", eng)


class Arena:
    def __init__(self, t, nbytes):
        self.t = t
        self.n = nbytes
        self.off = 0

    def reset(self, off=0):
        self.off = off

    def alloc(self, free_shape, dt):
        esz = 4 if dt == F32 else (2 if dt == BF16 else 1)
        n = esz
        for s in free_shape:
            n *= s
        off = (self.off + 63) // 64 * 64
        assert off + n <= self.n, ("SBUF arena overflow", off, n, self.n)
        self.off = off + n
        ap = self.t[:, off:off + n].bitcast(dt)
        if len(free_shape) == 2:
            ap = ap.rearrange("p (a b) -> p a b", a=free_shape[0])
        elif len(free_shape) == 3:
            ap = ap.rearrange("p (a b c) -> p a b c", a=free_shape[0], b=free_shape[1])
        elif len(free_shape) == 4:
            ap = ap.rearrange("p (a b c d) -> p a b c d", a=free_shape[0], b=free_shape[1], c=free_shape[2])
        return ap


class Ring:
    def __init__(self, arena, n, free_shape, dt, name="ring"):
        self.bufs = [arena.alloc(free_shape, dt) for _ in range(n)]
        self.res = [Res(name + str(i)) for i in range(n)]
        self.i = -1
        self.n = n

    def next(self):
        self.i = (self.i + 1) % self.n
        return self.bufs[self.i], self.res[self.i]


C_Q, C_K, C_V, C_RW, C_GATE, C_END = 0, 768, 1536, 2304, 5632, 7680
N_RW = 3328


def build(n_phase=99, dbg=False, skip=()):
    nc = bass.Bass("TRN2", target_bir_lowering=False)
    IN = lambda name, shape, dt=F32: nc.dram_tensor(name, shape, dt, kind="ExternalInput").ap()
    SCR = lambda name, shape, dt: nc.dram_tensor(name, shape, dt, kind=("ExternalOutput" if dbg else "Internal")).ap()
    x = IN("x", [S_LEN, D])
    w_in = IN("w_in", [D, C_END])
    ident_in = IN("ident", [128, 128])
    g_pre = IN("norm_mix_pre", [1, D])
    g_post = IN("norm_mix_post", [1, D])
    g_fpre = IN("norm_ffn_pre", [1, D])
    g_fpost = IN("norm_ffn_post", [1, D])
    mu_in = IN("mu_l", [128, 26])
    bg_in = IN("bgate_l", [128, 16])
    mbt_in = IN("mbt", [128, 12, 2, 128])
    out = nc.dram_tensor("out", [S_LEN, D], F32, kind="ExternalOutput").ap()

    qkT = SCR("qkT", [1536, S_LEN], BF16)
    vA = SCR("vA", [S_LEN, 768], BF16)
    frw = IN("frw_in", [N_RW, S_LEN]) if (dbg and 1 in skip) else SCR("frw", [N_RW, S_LEN], F32)
    gT = SCR("gT", [2048, S_LEN], BF16)
    oattT = SCR("oattT", [256, S_LEN], BF16)
    chv_in = IN("chv", [128, 7, 8])
    bones_in = IN("bones", [128, 128])
    bones64_in = IN("bones64", [128, 128])
    maskc_in = IN("maskc", [128, 512])
    maskA_in = IN("maskA", [64, 512])
    maskL_in = IN("maskL", [64, 128])
    ww2_in = IN("ww2", [64, D])
    wa2_in = IN("wa2", [64, D])
    wg2_in = IN("wg2", [128, D])
    wab_in = IN("wab", [256, D])
    wrb_in = IN("wrb", [D, D])
    wout_in = IN("wout", [D, D])
    w1_in = IN("w1", [D, 4096])
    w2_in = IN("w2", [4096, D])
    x1s = SCR("x1s", [S_LEN, D], F32)
    ARs = SCR("ARs", [64, 64, 8, 2, 128], BF16)
    BKs = SCR("BKs", [64, 64, 8, 2, 128], BF16)
    TKs = SCR("TKs", [64, 64, 8, 384], BF16)
    g2T = SCR("g2T", [D, S_LEN], BF16)
    bvT = SCR("bvT", [D, S_LEN], BF16)
    yT = SCR("yT", [D, S_LEN], F32)
    orwT = SCR("orwT", [D, S_LEN], BF16)

    with contextlib.ExitStack() as st:
        arena_t = st.enter_context(nc.sbuf_tensor("arena", [128, 204 * 1024], U8))[:, :]
        A = Arena(arena_t, 204 * 1024)
        banks = [st.enter_context(nc.psum_tensor("bank%d" % i, [128, 512], F32))[:, :] for i in range(8)]
        bank_res = [Res("bank%d" % i) for i in range(8)]
        S = Sched(nc)

        ident_f = A.alloc([128], F32)
        ident_b = A.alloc([128], BF16)
        r_ident = Res("ident")
        S.op("sp", lambda e: e.dma_start(out=ident_f, in_=ident_in), writes=[r_ident], dma=True)
        S.op("dve", lambda e: e.tensor_copy(out=ident_b, in_=ident_f), reads=[r_ident], writes=[r_ident])
        base_off = A.off

        def _pe_reset():
            S.op("pe", lambda e: e.matmul(banks[7][:, 0:128], lhsT=ident_b, rhs=ident_b, start=True, stop=True), reads=[r_ident, bank_res[7]], writes=[bank_res[7]])
        S.pe_reset = _pe_reset

        if n_phase >= 1 and 1 not in skip:
            hT = A.alloc([8, S_LEN], BF16)
            r_hT = [Res("hT%d" % i) for i in range(32)]
            gbc = A.alloc([D], F32)
            r_g = Res("gpre")
            S.op("sp", lambda e: e.dma_start(out=gbc, in_=g_pre.partition_broadcast(128)), writes=[r_g], dma=True)
            mu_t = A.alloc([26], F32)
            omm_t = A.alloc([26], F32)
            bg_t = A.alloc([16], F32)
            r_mu = Res("mu")
            S.op("sp", lambda e: e.dma_start(out=mu_t, in_=mu_in), writes=[r_mu], dma=True)
            S.op("sp", lambda e: e.dma_start(out=bg_t, in_=bg_in), writes=[r_mu], dma=True)
            S.op("dve", lambda e: e.tensor_scalar(out=omm_t, in0=mu_t, scalar1=-1.0, scalar2=1.0, op0=ALU.mult, op1=ALU.add), reads=[r_mu], writes=[r_mu])
            mark = A.off
            xt = Ring(A, 2, [D], F32, "xt")
            junk = A.alloc([D], BF16)
            r_junk = Res("junk")
            ssr = Ring(A, 2, [1], F32, "ss")
            rtr = Ring(A, 2, [1], F32, "rt")
            rsr = Ring(A, 2, [1], F32, "rstd")
            hbr = Ring(A, 2, [D], BF16, "hb")
            pb = 0
            for tt in range(32):
                xb, xr = xt.next()
                ss, ssres = ssr.next()
                rt, rtres = rtr.next()
                rs, rsres = rsr.next()
                hb, hbres = hbr.next()
                S.op("sp", lambda e, xb=xb, tt=tt: e.dma_start(out=xb, in_=x[tt * 128:(tt + 1) * 128, :]), writes=[xr], dma=True)
                S.op("act", lambda e, xb=xb, ss=ss: e.activation(out=junk, in_=xb, func=AF.Square, accum_out=ss), reads=[xr], writes=[r_junk, ssres])
                S.op("act", lambda e, ss=ss, rt=rt: e.activation(out=rt, in_=ss, func=AF.Sqrt, scale=1.0 / D, bias=1e-6), reads=[ssres], writes=[rtres])
                S.op("dve", lambda e, rt=rt, rs=rs: e.reciprocal(out=rs, in_=rt), reads=[rtres], writes=[rsres])
                S.op("dve", lambda e, xb=xb, rs=rs, hb=hb: e.scalar_tensor_tensor(out=hb, in0=xb, scalar=rs[:, 0:1], in1=gbc, op0=ALU.mult, op1=ALU.mult), reads=[xr, rsres, r_g], writes=[hbres])
                bk = banks[pb].bitcast(BF16).rearrange("p (a b) -> p a b", a=8)
                br = bank_res[pb]
                pb = (pb + 1) % 2
                for kc in range(8):
                    S.op("pe", lambda e, bk=bk, hb=hb, kc=kc: e.transpose(out=bk[:, kc, :], in_=hb[:, kc * 128:(kc + 1) * 128], identity=ident_b), reads=[hbres, r_ident], writes=[br])
                eng = "act" if tt % 2 == 0 else "dve"
                if eng == "act":
                    S.op("act", lambda e, bk=bk, tt=tt: e.copy(out=hT[:, :, tt * 128:(tt + 1) * 128], in_=bk), reads=[br], writes=[r_hT[tt]])
                else:
                    S.op("dve", lambda e, bk=bk, tt=tt: e.tensor_copy(out=hT[:, :, tt * 128:(tt + 1) * 128], in_=bk), reads=[br], writes=[r_hT[tt]])

            wst = Ring(A, 2, [8, 512], F32, "wst")
            wbf = Ring(A, 2, [8, 512], BF16, "wbf")
            praw = Ring(A, 3, [513], F32, "praw")
            tmpr = Ring(A, 2, [512], F32, "tmp")
            fo32 = Ring(A, 3, [512], F32, "fo32")
            fo16 = Ring(A, 3, [512], BF16, "fo16")
            w_view = w_in.rearrange("(kc p) c -> p kc c", p=128)
            pbk = [2]

            def nextbank():
                b = pbk[0]
                pbk[0] = 2 + (pbk[0] - 2 + 1) % 6
                return banks[b], bank_res[b]

            loaded = {}

            def load_w(cb):
                ws, wsr = wst.next()
                wb, wbr = wbf.next()
                S.op("sp", lambda e: e.dma_start(out=ws, in_=w_view[:, :, cb * 512:(cb + 1) * 512]), writes=[wsr], dma=True)
                S.op("pool", lambda e: e.tensor_copy(out=wb[:, 0:4, :], in_=ws[:, 0:4, :]), reads=[wsr], writes=[wbr])
                S.op("pool", lambda e: e.tensor_copy(out=wb[:, 4:8, :], in_=ws[:, 4:8, :]), reads=[wsr], writes=[wbr])
                loaded[cb] = (wb, wbr)

            NCB = 15
            load_w(0)
            for cb in range(NCB):
                if cb + 1 < NCB:
                    load_w(cb + 1)
                wb, wbr = loaded.pop(cb)
                vlo = max(C_V, cb * 512)
                vhi = min(C_RW, (cb + 1) * 512)
                if vlo < vhi:
                    n = vhi - vlo
                    lo = vlo - cb * 512
                    for tt in range(32):
                        ps, psr = nextbank()
                        for kc in range(8):
                            S.op("pe", lambda e, ps=ps, kc=kc, tt=tt, lo=lo, n=n, wb=wb: e.matmul(ps[:, 0:n], lhsT=hT[:, kc, tt * 128:(tt + 1) * 128], rhs=wb[:, kc, lo:lo + n], start=(kc == 0), stop=(kc == 7)), reads=[r_hT[tt], wbr], writes=[psr])
                        fo, fr_ = fo16.next()
                        S.op("act", lambda e, ps=ps, fo=fo, n=n: e.copy(out=fo[:, 0:n], in_=ps[:, 0:n]), reads=[psr], writes=[fr_])
                        S.op("pool", lambda e, fo=fo, n=n, tt=tt, vlo=vlo: e.dma_start(out=vA[tt * 128:(tt + 1) * 128, vlo - C_V:vlo - C_V + n], in_=fo[:, 0:n]), reads=[fr_], dma=True)
                for sb in range(4):
                    c0 = cb * 512 + sb * 128
                    if C_V <= c0 < C_RW:
                        continue
                    for tb in range(8):
                        ps, psr = nextbank()
                        for kc in range(8):
                            S.op("pe", lambda e, ps=ps, kc=kc, tb=tb, sb=sb, wb=wb: e.matmul(ps[:, :], lhsT=wb[:, kc, sb * 128:(sb + 1) * 128], rhs=hT[:, kc, tb * 512:(tb + 1) * 512], start=(kc == 0), stop=(kc == 7)), reads=[r_hT[4 * tb + i] for i in range(4)] + [wbr], writes=[psr])
                        if c0 < C_V:
                            fo, fr_ = fo16.next()
                            if tb % 2 == 0:
                                S.op("act", lambda e, ps=ps, fo=fo: e.copy(out=fo, in_=ps), reads=[psr], writes=[fr_])
                            else:
                                S.op("dve", lambda e, ps=ps, fo=fo: e.tensor_copy(out=fo, in_=ps), reads=[psr], writes=[fr_])
                            S.op("pool", lambda e, fo=fo, c0=c0, tb=tb: e.dma_start(out=qkT[c0:c0 + 128, tb * 512:(tb + 1) * 512], in_=fo), reads=[fr_], dma=True)
                        elif c0 >= C_GATE:
                            j = (c0 - C_GATE) // 128
                            fo, fr_ = fo16.next()
                            S.op("act", lambda e, ps=ps, fo=fo, j=j: e.activation(out=fo, in_=ps, func=AF.Sigmoid, bias=bg_t[:, j:j + 1]), reads=[psr, r_mu], writes=[fr_])
                            S.op("pool", lambda e, fo=fo, c0=c0, tb=tb: e.dma_start(out=gT[c0 - C_GATE:c0 - C_GATE + 128, tb * 512:(tb + 1) * 512], in_=fo), reads=[fr_], dma=True)
                        else:
                            j = (c0 - C_RW) // 128
                            if tb == 0:
                                pr, prr = praw.next()
                                S.op("pool", lambda e, pr=pr: e.memset(pr[:, 0:1], 0.0), writes=[prr])
                            else:
                                pr, prr = nxt
                            S.op("act", lambda e, ps=ps, pr=pr: e.copy(out=pr[:, 1:513], in_=ps), reads=[psr], writes=[prr])
                            if tb < 7:
                                nxt = praw.next()
                                S.op("pool", lambda e, pr=pr, np_=nxt[0]: e.tensor_copy(out=np_[:, 0:1], in_=pr[:, 512:513]), reads=[prr], writes=[nxt[1]])
                            tm, tmr = tmpr.next()
                            fo, fr_ = fo32.next()
                            S.op("dve", lambda e, pr=pr, tm=tm, j=j: e.tensor_scalar(out=tm, in0=pr[:, 0:512], scalar1=mu_t[:, j:j + 1], scalar2=None, op0=ALU.mult), reads=[prr, r_mu], writes=[tmr])
                            S.op("dve", lambda e, pr=pr, tm=tm, fo=fo, j=j: e.scalar_tensor_tensor(out=fo, in0=pr[:, 1:513], scalar=omm_t[:, j:j + 1], in1=tm, op0=ALU.mult, op1=ALU.add), reads=[prr, tmr, r_mu], writes=[fr_])
                            S.op("pool", lambda e, fo=fo, c0=c0, tb=tb: e.dma_start(out=frw[c0 - C_RW:c0 - C_RW + 128, tb * 512:(tb + 1) * 512], in_=fo), reads=[fr_], dma=True)
            S.barrier()
            A.reset(base_off)

        if n_phase >= 2 and 2 not in skip:
            mbt = A.alloc([12, 2, 128], F32)
            r_mbt = Res("mbt")
            S.op("sp", lambda e: e.dma_start(out=mbt, in_=mbt_in), writes=[r_mbt], dma=True)
            kq = Ring(A, 2, [2, S_LEN], BF16, "kq")
            vd = Ring(A, 2, [32, 128], BF16, "vd")
            acc = A.alloc([S_LEN], F32)
            r_acc = Res("acc")
            denlo = A.alloc([S_LEN], F32)
            r_den = Res("denlo")
            obf = A.alloc([S_LEN], BF16)
            r_obf = Res("obf")
            tmpS = Ring(A, 4, [128], F32, "tmpS")
            Ebr = Ring(A, 6, [128], BF16, "E")
            for i in range(2):
                S.op("pool", lambda e, b=vd.bufs[i]: e.memset(b[:, :, 64:128], 1.0), writes=[vd.res[i]])
            cnt2 = [0, 0]
            prev2 = [None]

            def tile2(g, d, head, kqb, kqr, vb, vr, ntile, r, mq):
                q0 = r + d * 128 * mq
                qsl = slice(q0, q0 + d * 127 + 1, d)
                kts = [mq - 1, mq] if mq > 0 else [mq]
                psO, psOr = banks[4 + cnt2[1] % 4], bank_res[4 + cnt2[1] % 4]
                cnt2[1] += 1
                Es = []
                for kt in kts:
                    k0 = r + d * 128 * kt
                    ksl = slice(k0, k0 + d * 127 + 1, d)
                    psS, psSr = banks[cnt2[0] % 4], bank_res[cnt2[0] % 4]
                    cnt2[0] += 1
                    S.op("pe", lambda e, psS=psS, kqb=kqb, ksl=ksl, qsl=qsl: e.matmul(psS[:, 0:128], lhsT=kqb[0:64, 0, ksl], rhs=kqb[0:64, 1, qsl], start=True, stop=True), reads=[kqr], writes=[psSr])
                    tm, tmr = tmpS.next()
                    which = 1 if kt == mq else 0
                    S.op("dve", lambda e, psS=psS, tm=tm, head=head, which=which: e.scalar_tensor_tensor(out=tm, in0=psS[:, 0:128], scalar=0.125, in1=mbt[:, head, which, :], op0=ALU.mult, op1=ALU.add), reads=[psSr, r_mbt], writes=[tmr])
                    Eb_, Er = Ebr.next()
                    S.op("act", lambda e, tm=tm, Eb_=Eb_: e.activation(out=Eb_, in_=tm, func=AF.Exp), reads=[tmr], writes=[Er])
                    Es.append((Eb_, Er, kt))
                yield
                for i, (Eb_, Er, kt) in enumerate(Es):
                    ti = r * ntile + kt
                    S.op("pe", lambda e, psO=psO, vb=vb, ti=ti, Eb_=Eb_, i=i, n=len(Es): e.matmul(psO[:, 0:128], lhsT=vb[:, ti, :], rhs=Eb_, start=(i == 0), stop=(i == n - 1)), reads=[vr, Er], writes=[psOr])
                if g == 0:
                    S.op("act", lambda e, psO=psO, qsl=qsl: e.copy(out=acc[:, qsl], in_=psO[:, 0:128]), reads=[psOr], writes=[r_acc])
                else:
                    S.op("dve", lambda e, psO=psO, qsl=qsl: e.tensor_tensor(out=acc[:, qsl], in0=psO[:, 0:128], in1=acc[:, qsl], op=ALU.add), reads=[psOr, r_acc], writes=[r_acc])

            for hh in range(4):
                for g, d in enumerate((1, 4, 16)):
                    head = g * 4 + hh
                    kqb, kqr = kq.next()
                    vb, vr = vd.next()
                    S.op("sp", lambda e, kqb=kqb, head=head: e.dma_start(out=kqb[0:64, 0, :], in_=qkT[768 + head * 64:768 + (head + 1) * 64, :]), writes=[kqr], dma=True)
                    S.op("sp", lambda e, kqb=kqb, head=head: e.dma_start(out=kqb[0:64, 1, :], in_=qkT[head * 64:(head + 1) * 64, :]), writes=[kqr], dma=True)
                    ntile = 32 // d
                    if d == 1:
                        for c4 in range(4):
                            S.op("sp", lambda e, vb=vb, head=head, c4=c4: e.dma_start(out=vb[:, c4 * 8:(c4 + 1) * 8, 0:64], in_=vA[c4 * 1024:(c4 + 1) * 1024, head * 64:(head + 1) * 64].rearrange("(mt p) c -> p mt c", p=128)), writes=[vr], dma=True)
                    else:
                        for r in range(d):
                            S.op("sp", lambda e, vb=vb, head=head, r=r, d=d, ntile=ntile: e.dma_start(out=vb[:, r * ntile:(r + 1) * ntile, 0:64], in_=vA[r:S_LEN:d, head * 64:(head + 1) * 64].rearrange("(mt p) c -> p mt c", p=128)), writes=[vr], dma=True)
                    for r in range(d):
                        for mq in range(ntile):
                            gnew = tile2(g, d, head, kqb, kqr, vb, vr, ntile, r, mq)
                            next(gnew)
                            if prev2[0] is not None:
                                next(prev2[0], None)
                            prev2[0] = gnew
                if prev2[0] is not None:
                    next(prev2[0], None)
                    prev2[0] = None
                S.op("sp", lambda e: e.dma_start(out=denlo[0:64, :], in_=acc[64:128, :]), reads=[r_acc], writes=[r_den], dma=True)
                S.op("act", lambda e: e.activation(out=denlo[0:64, :], in_=denlo[0:64, :], func=AF.Ln), reads=[r_den], writes=[r_den])
                S.op("act", lambda e: e.activation(out=denlo[0:64, :], in_=denlo[0:64, :], func=AF.Exp, scale=-1.0), reads=[r_den], writes=[r_den])
                S.op("dve", lambda e: e.tensor_tensor(out=obf[0:64, :], in0=acc[0:64, :], in1=denlo[0:64, :], op=ALU.mult), reads=[r_acc, r_den], writes=[r_obf])
                S.op("pool", lambda e, hh=hh: e.dma_start(out=oattT[hh * 64:(hh + 1) * 64, :], in_=obf[0:64, :]), reads=[r_obf], dma=True)
            S.barrier()
            A.reset(base_off)

        if n_phase >= 3 and 3 not in skip:
            C0 = 0.6065306597126334
            gam = A.alloc([8, 64], F32)
            r_gam = [Res("gam%d" % i) for i in range(8)]
            chv = A.alloc([7, 8], F32)
            omka = A.alloc([8], F32)
            r_chv = Res("chv")
            S.op("sp", lambda e: e.dma_start(out=chv, in_=chv_in), writes=[r_chv], dma=True)
            S.op("dve", lambda e: e.tensor_scalar(out=omka, in0=chv[:, 3, :], scalar1=-1.0, scalar2=1.0, op0=ALU.mult, op1=ALU.add), reads=[r_chv], writes=[r_chv])
            bones_f = A.alloc([128], F32)
            bones_b = A.alloc([128], BF16)
            r_bones = Res("bones")
            S.op("sp", lambda e: e.dma_start(out=bones_f, in_=bones_in), writes=[r_bones], dma=True)
            S.op("dve", lambda e: e.tensor_copy(out=bones_b, in_=bones_f), reads=[r_bones], writes=[r_bones])
            base3 = A.off
            maskc = A.alloc([512], F32)
            r_maskc = Res("maskc")
            S.op("sp", lambda e: e.dma_start(out=maskc, in_=maskc_in), writes=[r_maskc], dma=True)
            lst = Ring(A, 2, [S_LEN], F32, "lst")
            ww2 = A.alloc([D], BF16)
            wa2 = A.alloc([D], BF16)
            wg2 = A.alloc([D], BF16)
            tw = A.alloc([S_LEN], BF16)
            fab = A.alloc([S_LEN], BF16)
            sgb = A.alloc([S_LEN], BF16)
            r_lora = Res("lora")
            for (src, dst, np_, fn) in ((ww2_in, ww2, 64, None), (wa2_in, wa2, 64, None), (wg2_in, wg2, 128, None),
                                        (frw[3072:3136, :], tw, 64, AF.Tanh), (frw[3136:3200, :], fab, 64, AF.Copy), (frw[3200:3328, :], sgb, 128, AF.Sigmoid)):
                lb, lr = lst.next()
                n = src.shape[1]
                S.op("sp", lambda e, lb=lb, src=src, np_=np_, n=n: e.dma_start(out=lb[0:np_, 0:n], in_=src), writes=[lr], dma=True)
                if fn is None or fn == AF.Copy:
                    S.op("act", lambda e, lb=lb, dst=dst, np_=np_, n=n: e.copy(out=dst[0:np_, 0:n], in_=lb[0:np_, 0:n]), reads=[lr], writes=[r_lora])
                else:
                    S.op("act", lambda e, lb=lb, dst=dst, np_=np_, n=n, fn=fn: e.activation(out=dst[0:np_, 0:n], in_=lb[0:np_, 0:n], func=fn), reads=[lr], writes=[r_lora])
            NR = 2
            fin = [Ring(A, 3, [512], F32, "fin%d" % i) for i in range(3)]
            T32 = [Ring(A, NR, [512], F32, "t32_%d" % i) for i in range(16)]
            T16 = [Ring(A, NR, [512], BF16, "t16_%d" % i) for i in range(6)]
            ARo_r = Ring(A, NR, [8, 2, 64], BF16, "ARo")
            BKo_r = Ring(A, NR, [8, 2, 64], BF16, "BKo")
            TKo_r = Ring(A, NR, [8, 3, 128], BF16, "TKo")
            pbk3 = [0]

            def nb3():
                b = pbk3[0]
                pbk3[0] = (b + 1) % 8
                return banks[b], bank_res[b]

            def v3(ap):
                return ap.rearrange("p (c t) -> p c t", c=8)

            def p3_load(ct, tb):
                tsl = slice(tb * 512, (tb + 1) * 512)
                bufs = [f.next() for f in fin]
                for i, (buf, res) in enumerate(bufs):
                    S.op("sp", lambda e, buf=buf, i=i, tsl=tsl, ct=ct: e.dma_start(out=buf, in_=frw[i * 1024 + ct * 128:i * 1024 + (ct + 1) * 128, tsl]), writes=[res], dma=True)
                return bufs

            blocks3 = [(ct, tb) for ct in range(8) for tb in range(8)]

            def block3(bi, ct, tb, bufs):
                tsl = slice(tb * 512, (tb + 1) * 512)
                csl = slice(ct * 128, (ct + 1) * 128)
                (fr_, fr_r), (fk_, fk_r), (fv_, fv_r) = bufs
                t = [r.next() for r in T32]
                h = [r.next() for r in T16]
                (sgm, sgm_r), (a_, a_r), (k1, k1_r), (nr, nr_r), (kk, kk_r), (t1, t1_r), (k2, k2_r), (b_, b_r) = t[0:8]
                (cs, cs_r), (tmp1, tmp1_r), (tmp2, tmp2_r), (einc, einc_r), (eneg, eneg_r), (eexc, eexc_r), (eend, eend_r), (rn, rn_r) = t[8:16]
                (g_o, g_or), (sq, sq_r), (rk, rk_r), (bv, bv_r), (khat, khat_r), (bhat, bhat_r) = h
                ps_w, ps_wr = nb3()
                ps_a, ps_ar = nb3()
                ps_g, ps_gr = nb3()
                S.op("pe", lambda e, ps_w=ps_w, csl=csl, tsl=tsl: e.matmul(ps_w, lhsT=ww2[0:64, csl], rhs=tw[0:64, tsl], start=True, stop=True), reads=[r_lora], writes=[ps_wr])
                S.op("pe", lambda e, ps_a=ps_a, csl=csl, tsl=tsl: e.matmul(ps_a, lhsT=wa2[0:64, csl], rhs=fab[0:64, tsl], start=True, stop=True), reads=[r_lora], writes=[ps_ar])
                S.op("pe", lambda e, ps_g=ps_g, csl=csl, tsl=tsl: e.matmul(ps_g, lhsT=wg2[:, csl], rhs=sgb[:, tsl], start=True, stop=True), reads=[r_lora], writes=[ps_gr])
                S.op("act", lambda e, ps_w=ps_w, sgm=sgm, ct=ct: e.activation(out=sgm, in_=ps_w, func=AF.Sigmoid, bias=chv[:, 0, ct:ct + 1]), reads=[ps_wr, r_chv], writes=[sgm_r])
                S.op("act", lambda e, ps_a=ps_a, a_=a_, ct=ct: e.activation(out=a_, in_=ps_a, func=AF.Sigmoid, bias=chv[:, 1, ct:ct + 1]), reads=[ps_ar, r_chv], writes=[a_r])
                S.op("dve", lambda e, ps_g=ps_g, g_o=g_o: e.tensor_copy(out=g_o, in_=ps_g), reads=[ps_gr], writes=[g_or])
                S.op("sp", lambda e, g_o=g_o, csl=csl, tsl=tsl: e.dma_start(out=g2T[csl, tsl], in_=g_o), reads=[g_or], dma=True)
                S.op("act", lambda e, k1=k1, fk_=fk_, ct=ct: e.activation(out=k1, in_=fk_, func=AF.Copy, scale=chv[:, 2, ct:ct + 1]), reads=[fk_r, r_chv], writes=[k1_r])
                S.op("act", lambda e, fk_=fk_, sq=sq, ct=ct: e.activation(out=sq, in_=fk_, func=AF.Square, scale=chv[:, 2, ct:ct + 1]), reads=[fk_r, r_chv], writes=[sq_r])
                ps_n, ps_nr = nb3()
                S.op("pe", lambda e, ps_n=ps_n, sq=sq: e.matmul(ps_n, lhsT=bones_b, rhs=sq, start=True, stop=True), reads=[sq_r, r_bones], writes=[ps_nr])
                yield
                S.op("act", lambda e, ps_n=ps_n, nr=nr: e.activation(out=nr, in_=ps_n, func=AF.Ln, scale=float(2.0 ** 40)), reads=[ps_nr], writes=[nr_r])
                S.op("act", lambda e, nr=nr, rn=rn: e.activation(out=rn, in_=nr, func=AF.Exp, scale=-0.5, bias=13.862943611198906), reads=[nr_r], writes=[rn_r])
                S.op("dve", lambda e, kk=kk, k1=k1, rn=rn: e.scalar_tensor_tensor(out=kk, in0=rn, scalar=1e12, in1=k1, op0=ALU.min, op1=ALU.mult), reads=[k1_r, rn_r], writes=[kk_r])
                S.op("dve", lambda e, t1=t1, a_=a_, ct=ct: e.tensor_scalar(out=t1, in0=a_, scalar1=chv[:, 3, ct:ct + 1], scalar2=omka[:, ct:ct + 1], op0=ALU.mult, op1=ALU.add), reads=[a_r, r_chv], writes=[t1_r])
                S.op("dve", lambda e, k2=k2, fk_=fk_, t1=t1: e.tensor_tensor(out=k2, in0=fk_, in1=t1, op=ALU.mult), reads=[fk_r, t1_r], writes=[k2_r])
                S.op("pool", lambda e, b_=b_, kk=kk, a_=a_: e.tensor_tensor(out=b_, in0=kk, in1=a_, op=ALU.mult), reads=[kk_r, a_r], writes=[b_r])
                S.op("dve", lambda e, rk=rk, fr_=fr_, k2=k2, ct=ct: e.scalar_tensor_tensor(out=rk, in0=fr_, scalar=chv[:, 4, ct:ct + 1], in1=k2, op0=ALU.mult, op1=ALU.mult), reads=[fr_r, k2_r, r_chv], writes=[rk_r])
                ps_b, ps_br = nb3()
                S.op("pe", lambda e, ps_b=ps_b, rk=rk: e.matmul(ps_b, lhsT=bones_b, rhs=rk, start=True, stop=True), reads=[rk_r, r_bones], writes=[ps_br])
                S.op("dve", lambda e, ps_b=ps_b, bv=bv, fv_=fv_: e.tensor_tensor(out=bv, in0=ps_b, in1=fv_, op=ALU.mult), reads=[ps_br, fv_r], writes=[bv_r])
                S.op("sp", lambda e, bv=bv, csl=csl, tsl=tsl: e.dma_start(out=bvT[csl, tsl], in_=bv), reads=[bv_r], dma=True)
                yield
                S.op("dve", lambda e, cs=cs, sgm=sgm: e.tensor_tensor_scan(out=cs, data0=maskc, data1=sgm, initial=0.0, op0=ALU.mult, op1=ALU.add), reads=[sgm_r, r_maskc], writes=[cs_r])
                S.op("pool", lambda e, tmp1=tmp1, cs=cs, sgm=sgm: e.tensor_tensor(out=tmp1, in0=cs, in1=sgm, op=ALU.subtract), reads=[cs_r, sgm_r], writes=[tmp1_r])
                S.op("act", lambda e, einc=einc, cs=cs: e.activation(out=einc, in_=cs, func=AF.Exp, scale=-C0), reads=[cs_r], writes=[einc_r])
                S.op("act", lambda e, eneg=eneg, cs=cs: e.activation(out=eneg, in_=cs, func=AF.Exp, scale=C0), reads=[cs_r], writes=[eneg_r])
                S.op("act", lambda e, eexc=eexc, tmp1=tmp1: e.activation(out=eexc, in_=tmp1, func=AF.Exp, scale=-C0), reads=[tmp1_r], writes=[eexc_r])
                yield
                ARo, ARo_res = ARo_r.next()
                BKo, BKo_res = BKo_r.next()
                TKo, TKo_res = TKo_r.next()
                S.op("dve", lambda e, ARo=ARo, fr_=fr_, einc=einc: e.tensor_tensor(out=ARo[:, :, 1, :], in0=v3(fr_), in1=v3(einc), op=ALU.mult), reads=[fr_r, einc_r], writes=[ARo_res])
                S.op("dve", lambda e, ARo=ARo, kk=kk, eexc=eexc: e.scalar_tensor_tensor(out=ARo[:, :, 0, :], in0=v3(kk), scalar=-1.0, in1=v3(eexc), op0=ALU.mult, op1=ALU.mult), reads=[kk_r, eexc_r], writes=[ARo_res])
                S.op("pool", lambda e, BKo=BKo, b_=b_, eneg=eneg: e.tensor_tensor(out=BKo[:, :, 0, :], in0=v3(b_), in1=v3(eneg), op=ALU.mult), reads=[b_r, eneg_r], writes=[BKo_res])
                S.op("pool", lambda e, BKo=BKo, k2=k2, eneg=eneg: e.tensor_tensor(out=BKo[:, :, 1, :], in0=v3(k2), in1=v3(eneg), op=ALU.mult), reads=[k2_r, eneg_r], writes=[BKo_res])
                S.op("dve", lambda e, khat=khat, BKo=BKo, einc=einc: e.tensor_tensor(out=v3(khat), in0=BKo[:, :, 1, :], in1=v3(einc)[:, :, 63:64].to_broadcast([128, 8, 64]), op=ALU.mult), reads=[BKo_res, einc_r], writes=[khat_r])
                S.op("pool", lambda e, bhat=bhat, BKo=BKo, einc=einc: e.tensor_tensor(out=v3(bhat), in0=BKo[:, :, 0, :], in1=v3(einc)[:, :, 63:64].to_broadcast([128, 8, 64]), op=ALU.mult), reads=[BKo_res, einc_r], writes=[bhat_r])
                S.op("pool", lambda e, sq=sq, fv_=fv_: e.tensor_copy(out=sq, in_=fv_), reads=[fv_r], writes=[sq_r])
                S.op("pool", lambda e, einc=einc, ct=ct, tb=tb: e.tensor_copy(out=gam[:, ct, tb * 8:(tb + 1) * 8], in_=v3(einc)[:, :, 63]), reads=[einc_r], writes=[r_gam[ct]])
                yield
                for j, (src, src_r) in enumerate(((khat, khat_r), (bhat, bhat_r), (sq, sq_r))):
                    psT, psT_r = nb3()
                    pv = psT.bitcast(BF16).rearrange("p (c x) -> p c x", c=8)
                    for ch in range(8):
                        S.op("pe", lambda e, pv=pv, src=src, ch=ch: e.transpose(out=pv[0:64, ch, :], in_=src[:, ch * 64:(ch + 1) * 64], identity=ident_b), reads=[src_r, r_ident], writes=[psT_r])
                    if j <= 1:
                        S.op("dve", lambda e, pv=pv, TKo=TKo, j=j: e.tensor_copy(out=TKo[0:64, :, j, :], in_=pv[0:64, :, :]), reads=[psT_r], writes=[TKo_res])
                    else:
                        S.op("act", lambda e, pv=pv, TKo=TKo, j=j: e.copy(out=TKo[0:64, :, j, :], in_=pv[0:64, :, :]), reads=[psT_r], writes=[TKo_res])
                for hd in range(2):
                    S.op("sp", lambda e, ARo=ARo, ct=ct, tb=tb, hd=hd: e.dma_start(out=ARs[tb * 8:(tb + 1) * 8, :, ct, hd, :].rearrange("c p x -> p c x"), in_=ARo[hd * 64:(hd + 1) * 64].rearrange("p c a x -> p c (a x)")), reads=[ARo_res], dma=True)
                    S.op("sp", lambda e, BKo=BKo, ct=ct, tb=tb, hd=hd: e.dma_start(out=BKs[tb * 8:(tb + 1) * 8, :, ct, hd, :].rearrange("c p x -> p c x"), in_=BKo[hd * 64:(hd + 1) * 64].rearrange("p c a x -> p c (a x)")), reads=[BKo_res], dma=True)
                S.op("sp", lambda e, TKo=TKo, ct=ct, tb=tb: e.dma_start(out=TKs[tb * 8:(tb + 1) * 8, :, ct, :].rearrange("c p x -> p c x"), in_=TKo[0:64].rearrange("p c a x -> p c (a x)")), reads=[TKo_res], dma=True)

            n3 = len(blocks3)
            pre3 = {0: p3_load(*blocks3[0]), 1: p3_load(*blocks3[1])}
            g3 = {0: block3(0, *blocks3[0], pre3.pop(0))}
            next(g3[0]); next(g3[0]); next(g3[0])
            for bi in range(n3):
                if bi + 2 < n3:
                    pre3[bi + 2] = p3_load(*blocks3[bi + 2])
                nx = None
                if bi + 1 < n3:
                    nx = g3[bi + 1] = block3(bi + 1, *blocks3[bi + 1], pre3.pop(bi + 1))
                cur = g3.pop(bi)
                if nx is not None:
                    next(nx)
                next(cur)
                if nx is not None:
                    next(nx)
                next(cur, None)
                if nx is not None:
                    next(nx)
            S.barrier()
            A.reset(base3)

        if n_phase >= 4 and 4 not in skip:
            if 3 in skip:
                gam = A.alloc([8, 64], F32)
                r_gam = [Res('g') for _ in range(8)]
                base3 = A.off
            S.op('pe', lambda e: e.matmul(banks[7][:, 0:128], lhsT=ident_b, rhs=ident_b, start=True, stop=True), reads=[r_ident], writes=[bank_res[7]])
            maskA = A.alloc([512], F32)
            maskL = A.alloc([128], F32)
            r_mk = Res("masks")
            S.op("sp", lambda e: e.dma_start(out=maskA[0:64, :], in_=maskA_in), writes=[r_mk], dma=True)
            S.op("sp", lambda e: e.dma_start(out=maskL[0:64, :], in_=maskL_in), writes=[r_mk], dma=True)
            ident2 = A.alloc([2, 64], BF16)
            r_id2 = Res("ident2")
            for hd in range(2):
                S.op("pool", lambda e, hd=hd: e.tensor_copy(out=ident2[0:64, hd, :], in_=ident_b[0:64, 0:64]), reads=[r_ident], writes=[r_id2])
            gam2 = A.alloc([8, 2, 64], F32)
            r_gam2 = Res("gam2")
            S.op("sp", lambda e: e.dma_start(out=gam2[0:64, :, 0, :], in_=gam[0:64, :, :]), reads=r_gam, writes=[r_gam2], dma=True)
            S.op("sp", lambda e: e.dma_start(out=gam2[0:64, :, 1, :], in_=gam[64:128, :, :]), reads=r_gam, writes=[r_gam2], dma=True)
            ST = A.alloc([8, 2, 64], F32)
            STb = A.alloc([8, 2, 64], BF16)
            r_ST = [Res("ST%d" % i) for i in range(8)]
            r_STb = [Res("STb%d" % i) for i in range(8)]
            S.op("pool", lambda e: e.memset(ST, 0.0), writes=r_ST)
            S.op("pool", lambda e: e.memset(STb, 0.0), writes=r_STb)
            NCH = 3
            ARc = Ring(A, NCH, [8, 2, 2, 64], BF16, "ARc")
            BKc = Ring(A, NCH, [8, 2, 2, 64], BF16, "BKc")
            TKc = Ring(A, NCH, [8, 3, 128], BF16, "TKc")
            NW = 16
            AAr = Ring(A, NW, [2, 2, 128], BF16, "AA")
            NTr = Ring(A, NW, [2, 64], BF16, "NT")
            PPr = Ring(A, NW, [2, 2, 64], BF16, "PP")
            Tr = Ring(A, 2 * NW, [2, 64], BF16, "T")
            XTr = Ring(A, NW, [2, 64], BF16, "XT")
            UTr = Ring(A, NW, [2, 64], BF16, "UT")
            Yb = Ring(A, 2, [8, 2, 512], F32, "Yb")

            class PRing:
                def __init__(self, specs):
                    self.aps = [banks[b][:, lo:lo + n] for (b, lo, n) in specs]
                    self.res = [bank_res[b] for (b, lo, n) in specs]
                    self.i = -1

                def next(self):
                    self.i = (self.i + 1) % len(self.aps)
                    return self.aps[self.i], self.res[self.i]

            pA = PRing([(b, 0, 512) for b in (0, 1, 2, 3)])
            pA2 = PRing([(b, 0, 128) for b in (4, 5, 6, 7)])
            pP = PRing([(b, 0, 256) for b in (0, 1, 2, 3)])
            pT = PRing([(b, 0, 128) for b in (4, 5, 6, 7)])
            pX = PRing([(b, 0, 128) for b in (0, 1)])
            pU = PRing([(b, 0, 128) for b in (2, 3)])
            pY = PRing([(b, 0, 128) for b in (4, 5)])
            pS = PRing([(b, 0, 128) for b in (6, 7)])

            def v22(ap):
                return ap.rearrange("p (h a x) -> p h a x", h=2, a=2)

            def v2(ap, n):
                return ap.rearrange("p (h x) -> p h x", h=2)

            chunk_bufs = {}

            def load_chunk(c):
                (ar, ar_r), (bk, bk_r), (tk, tk_r) = ARc.next(), BKc.next(), TKc.next()
                S.op("sp", lambda e: e.dma_start(out=ar[0:64].rearrange("p c h a x -> p (c h a x)"), in_=ARs[c].rearrange("p c h x -> p (c h x)")), writes=[ar_r], dma=True)
                S.op("sp", lambda e: e.dma_start(out=bk[0:64].rearrange("p c h a x -> p (c h a x)"), in_=BKs[c].rearrange("p c h x -> p (c h x)")), writes=[bk_r], dma=True)
                S.op("sp", lambda e: e.dma_start(out=tk[0:64].rearrange("p c a x -> p (c a x)"), in_=TKs[c].rearrange("p c x -> p (c x)")), writes=[tk_r], dma=True)
                chunk_bufs[c] = (ar, ar_r, bk, bk_r, tk, tk_r)

            load_chunk(0)
            load_chunk(1)
            import os as _os

            def do_chunk(c, ar, ar_r, bk, bk_r, tk, tk_r, yb, yb_r):
                W = []
                if int(_os.environ.get('DBG_STEP', 9)) < 1:
                    return
                for ct in range(8):
                    psA, psA_r = pA.next()
                    psA2, psA2_r = pA2.next()
                    a4 = v22(psA)
                    a2 = v2(psA2, 64)
                    for hd in range(2):
                        p0 = hd * 64
                        S.op("pe", lambda e, a4=a4, hd=hd, p0=p0, ct=ct: e.matmul(a4[0:64, hd, 0, :], lhsT=bk[0:64, ct, hd, 0, :], rhs=ar[0:64, ct, hd, :, :].rearrange("p a x -> p (a x)"), start=True, stop=True), reads=[bk_r, ar_r], writes=[psA_r])
                        S.op("pe", lambda e, a4=a4, hd=hd, p0=p0, ct=ct: e.matmul(a4[0:64, hd, 1, :], lhsT=bk[0:64, ct, hd, 1, :], rhs=ar[0:64, ct, hd, :, :].rearrange("p a x -> p (a x)"), start=True, stop=True), reads=[bk_r, ar_r], writes=[psA_r])
                        S.op("pe", lambda e, a2=a2, hd=hd, p0=p0, ct=ct: e.matmul(a2[0:64, hd, :], lhsT=ar[0:64, ct, hd, 0, :], rhs=bk[0:64, ct, hd, 0, :], start=True, stop=True), reads=[bk_r, ar_r], writes=[psA2_r])
                    AA, AA_r = AAr.next()
                    NT, NT_r = NTr.next()
                    T0, T0_r = Tr.next()
                    _sub = int(_os.environ.get('DBG_SUB', 9))
                    if _sub >= 2:
                      S.op("dve", lambda e, psA=psA, AA=AA: e.tensor_tensor(out=AA[0:64].rearrange("p h a x -> p (h a x)"), in0=psA[0:64, :], in1=maskA[0:64, :], op=ALU.mult), reads=[r_mk], writes=[AA_r, psA_r])
                    if _sub >= 3:
                      S.op("dve", lambda e, psA2=psA2, NT=NT: e.tensor_tensor(out=NT[0:64].rearrange("p h x -> p (h x)"), in0=psA2[0:64, :], in1=maskL[0:64, :], op=ALU.mult), reads=[r_mk], writes=[NT_r, psA2_r])
                    if _sub >= 4:
                      S.op("pool", lambda e, AA=AA, T0=T0: e.tensor_tensor(out=T0[0:64], in0=AA[0:64, :, 0, 0:64], in1=ident2[0:64], op=ALU.add), reads=[AA_r, r_id2], writes=[T0_r])
                    if int(_os.environ.get('DBG_DELAY', 0)):
                        S.op("dve", lambda e, AA=AA: e.tensor_scalar(out=AA[0:64].rearrange("p h a x -> p (h a x)"), in0=AA[0:64].rearrange("p h a x -> p (h a x)"), scalar1=1.0, scalar2=None, op0=ALU.mult), reads=[r_mk], writes=[AA_r])
                        S.op("dve", lambda e, NT=NT: e.tensor_scalar(out=NT[0:64].rearrange("p h x -> p (h x)"), in0=NT[0:64].rearrange("p h x -> p (h x)"), scalar1=1.0, scalar2=None, op0=ALU.mult), reads=[r_mk], writes=[NT_r])
                    if int(_os.environ.get('DBG_CONST', 0)):
                        S.op("dve", lambda e, AA=AA: e.tensor_scalar(out=AA[0:64].rearrange("p h a x -> p (h a x)"), in0=maskA[0:64, :], scalar1=0.01, scalar2=None, op0=ALU.mult), reads=[r_mk], writes=[AA_r])
                        S.op("dve", lambda e, NT=NT: e.tensor_scalar(out=NT[0:64].rearrange("p h x -> p (h x)"), in0=maskL[0:64, :], scalar1=0.01, scalar2=None, op0=ALU.mult), reads=[r_mk], writes=[NT_r])
                    W.append(dict(AA=AA, AA_r=AA_r, P=(lambda AA=AA: AA[0:64, :, 0, 0:64]), PT=(lambda NT=NT: NT[0:64]), P_r=AA_r, PT_r=NT_r, T=T0, T_r=T0_r))
                yield
                _stp = int(_os.environ.get('DBG_STEP', 9))
                if _stp < 2:
                    return
                for lvl in range(int(_os.environ.get('DBG_LVL', 5))):
                    for ct in range(8):
                        w = W[ct]
                        psP, psP_r = pP.next()
                        p4 = psP.rearrange("p (h a x) -> p h a x", h=2, a=2)
                        Pp, PTp = w["P"](), w["PT"]()
                        for hd in range(2):
                            if lvl < 4:
                                S.op("pe", lambda e, p4=p4, hd=hd, Pp=Pp, PTp=PTp: e.matmul(p4[0:64, hd, 0, :], lhsT=PTp[:, hd, :], rhs=Pp[:, hd, :], start=True, stop=True), reads=[w["P_r"], w["PT_r"]], writes=[psP_r])
                            S.op("pe", lambda e, p4=p4, hd=hd, Pp=Pp, PTp=PTp: e.matmul(p4[0:64, hd, 1, :], lhsT=Pp[:, hd, :], rhs=PTp[:, hd, :], start=True, stop=True), reads=[w["P_r"], w["PT_r"]], writes=[psP_r])
                        PP, PP_r = PPr.next()
                        if int(_os.environ.get('DBG_NOCOPY', 0)):
                            pass
                        elif lvl < 4:
                            S.op("act", lambda e, psP=psP, PP=PP: e.copy(out=PP[0:64].rearrange("p h a x -> p (h a x)"), in_=psP[0:64, :]), reads=[], writes=[PP_r, psP_r])
                        else:
                            S.op("act", lambda e, p4=p4, PP=PP: e.copy(out=PP[0:64, :, 1, :], in_=p4[0:64, :, 1, :]), reads=[], writes=[PP_r, psP_r])
                        w["P"] = (lambda PP=PP: PP[0:64, :, 0, :])
                        w["PT"] = (lambda PP=PP: PP[0:64, :, 1, :])
                        w["P_r"] = PP_r
                        w["PT_r"] = PP_r
                    for ct in range(8 if int(_os.environ.get('DBG_T', 1)) else 0):
                        w = W[ct]
                        psT, psT_r = pT.next()
                        t2 = v2(psT, 64)
                        PTn = w["PT"]()
                        Tp = w["T"]
                        for hd in range(2):
                            S.op("pe", lambda e, t2=t2, hd=hd, PTn=PTn, Tp=Tp: e.matmul(t2[0:64, hd, :], lhsT=PTn[:, hd, :], rhs=Tp[0:64, hd, :], start=True, stop=True), reads=[w["PT_r"], w["T_r"]], writes=[psT_r])
                        Tn, Tn_r = Tr.next()
                        S.op("dve", lambda e, psT=psT, Tn=Tn, Tp=Tp: e.tensor_tensor(out=Tn[0:64].rearrange("p h x -> p (h x)"), in0=psT[0:64, :], in1=Tp[0:64].rearrange("p h x -> p (h x)"), op=ALU.add), reads=[w["T_r"]], writes=[Tn_r, psT_r])
                        w["T"] = Tn
                        w["T_r"] = Tn_r
                    yield
                if _stp < 3:
                    return
                for ct in range(8):
                    w = W[ct]
                    psX, psX_r = pX.next()
                    x2 = v2(psX, 64)
                    AA = w["AA"]
                    for hd in range(2):
                        p0 = hd * 64
                        S.op("pe", lambda e, x2=x2, hd=hd, p0=p0, ct=ct: e.matmul(x2[0:64, hd, :], lhsT=ar[0:64, ct, hd, 0, :], rhs=STb[0:64, ct, hd, :], start=True, stop=False), reads=[ar_r, r_STb[ct]], writes=[psX_r])
                        S.op("pe", lambda e, x2=x2, hd=hd, AA=AA, ct=ct: e.matmul(x2[0:64, hd, :], lhsT=AA[0:64, hd, 1, 0:64], rhs=tk[0:64, ct, 2, hd * 64:(hd + 1) * 64], start=False, stop=True), reads=[w["AA_r"], tk_r], writes=[psX_r])
                    XT, XT_r = XTr.next()
                    S.op("act", lambda e, psX=psX, XT=XT: e.copy(out=XT[0:64].rearrange("p h x -> p (h x)"), in_=psX[0:64, :]), reads=[], writes=[XT_r, psX_r])
                    w["XT"], w["XT_r"] = XT, XT_r
                yield
                if _stp < 4:
                    return
                for ct in range(8):
                    w = W[ct]
                    psU, psU_r = pU.next()
                    u2 = v2(psU, 64)
                    Tf, XT = w["T"], w["XT"]
                    for hd in range(2):
                        S.op("pe", lambda e, u2=u2, hd=hd, Tf=Tf, XT=XT: e.matmul(u2[0:64, hd, :], lhsT=Tf[0:64, hd, :], rhs=XT[0:64, hd, :], start=True, stop=True), reads=[w["T_r"], w["XT_r"]], writes=[psU_r])
                    UT, UT_r = UTr.next()
                    S.op("act", lambda e, psU=psU, UT=UT: e.copy(out=UT[0:64].rearrange("p h x -> p (h x)"), in_=psU[0:64, :]), reads=[], writes=[UT_r, psU_r])
                    w["UT"], w["UT_r"] = UT, UT_r
                yield
                if _stp < 5:
                    return
                for ct in range(8):
                    w = W[ct]
                    psY_, psY_r = pY.next()
                    psY = v2(psY_, 64)
                    AA, UT = w["AA"], w["UT"]
                    for hd in range(2):
                        p0 = hd * 64
                        S.op("pe", lambda e, psY=psY, p0=p0, ct=ct, hd=hd: e.matmul(psY[0:64, hd, :], lhsT=STb[0:64, ct, hd, :], rhs=ar[0:64, ct, hd, 1, :], start=True, stop=False), reads=[r_STb[ct], ar_r], writes=[psY_r])
                        S.op("pe", lambda e, psY=psY, p0=p0, ct=ct, hd=hd, AA=AA: e.matmul(psY[0:64, hd, :], lhsT=tk[0:64, ct, 2, hd * 64:(hd + 1) * 64], rhs=AA[0:64, hd, 1, 64:128], start=False, stop=False), reads=[tk_r, w["AA_r"]], writes=[psY_r])
                        S.op("pe", lambda e, psY=psY, p0=p0, hd=hd, AA=AA, UT=UT: e.matmul(psY[0:64, hd, :], lhsT=UT[0:64, hd, :], rhs=AA[0:64, hd, 0, 64:128], start=False, stop=True), reads=[w["UT_r"], w["AA_r"]], writes=[psY_r])
                    S.op("act", lambda e, psY=psY, yb=yb, ct=ct, c=c: e.copy(out=yb[0:64, ct, :, (c % 8) * 64:(c % 8 + 1) * 64], in_=psY[0:64]), reads=[], writes=[yb_r, psY_r])
                    psS_, psS_r = pS.next()
                    psS = v2(psS_, 64)
                    for hd in range(2):
                        p0 = hd * 64
                        S.op("pe", lambda e, psS=psS, p0=p0, ct=ct, hd=hd: e.matmul(psS[0:64, hd, :], lhsT=tk[0:64, ct, 0, hd * 64:(hd + 1) * 64], rhs=tk[0:64, ct, 2, hd * 64:(hd + 1) * 64], start=True, stop=False), reads=[tk_r], writes=[psS_r])
                        S.op("pe", lambda e, psS=psS, p0=p0, ct=ct, hd=hd, UT=UT: e.matmul(psS[0:64, hd, :], lhsT=tk[0:64, ct, 1, hd * 64:(hd + 1) * 64], rhs=UT[0:64, hd, :], start=False, stop=True), reads=[tk_r, w["UT_r"]], writes=[psS_r])
                    for hd in range(2):
                        S.op("dve", lambda e, psS=psS, ct=ct, c=c, hd=hd: e.scalar_tensor_tensor(out=ST[0:64, ct, hd, :], in0=ST[0:64, ct, hd, :], scalar=gam2[0:64, ct, hd, c:c + 1], in1=psS[0:64, hd, :], op0=ALU.mult, op1=ALU.add), reads=[r_gam2], writes=[r_ST[ct], psS_r])
                    S.op("pool", lambda e, ct=ct: e.tensor_copy(out=STb[0:64, ct, :, :], in_=ST[0:64, ct, :, :]), reads=[r_ST[ct]], writes=[r_STb[ct]])
                if c % 8 == 7:
                    tb = c // 8
                    S.op("pool", lambda e, yb=yb, tb=tb: e.dma_start(out=yT.rearrange("(ct hd p) t -> p ct hd t", hd=2, p=64)[:, :, :, tb * 512:(tb + 1) * 512], in_=yb[0:64]), reads=[yb_r], dma=True)

            nch = int(_os.environ.get('DBG_NCH', 64))
            ybs = {}

            def mk(c):
                if c % 8 == 0:
                    ybs[c // 8] = Yb.next()
                yb, yb_r = ybs[c // 8]
                return do_chunk(c, *chunk_bufs.pop(c), yb, yb_r)

            g4 = {0: mk(0)}
            for _ in range(6):
                next(g4[0])
            for c in range(nch):
                if c + 2 < 64:
                    load_chunk(c + 2)
                nx = None
                if c + 1 < nch:
                    nx = g4[c + 1] = mk(c + 1)
                cur = g4.pop(c)
                if nx is not None:
                    next(nx)
                next(cur)
                if nx is not None:
                    next(nx)
                next(cur)
                if nx is not None:
                    next(nx)
                next(cur, None)
                if nx is not None:
                    next(nx); next(nx); next(nx)
            S.barrier()
            A.reset(base_off)

        if n_phase >= 5 and 5 not in skip:
            chv5 = A.alloc([7, 8], F32)
            b64 = A.alloc([128], F32)
            r_c5 = Res("c5")
            S.op("sp", lambda e: e.dma_start(out=chv5, in_=chv_in), writes=[r_c5], dma=True)
            S.op("sp", lambda e: e.dma_start(out=b64, in_=bones64_in), writes=[r_c5], dma=True)
            yin = Ring(A, 2, [512], F32, "yin")
            bvin = Ring(A, 2, [512], BF16, "bvin")
            g2in = Ring(A, 2, [512], BF16, "g2in")
            W5 = [Ring(A, 2, [512], F32, "w5_%d" % i) for i in range(6)]
            oo = Ring(A, 2, [512], BF16, "oo")
            pb5 = [0]

            def nb5():
                b = pb5[0]
                pb5[0] = (b + 1) % 8
                return banks[b], bank_res[b]

            def block5(ct, tb):
                tsl = slice(tb * 512, (tb + 1) * 512)
                csl = slice(ct * 128, (ct + 1) * 128)
                y_, y_r = yin.next()
                bv_, bv_r = bvin.next()
                g2_, g2_r = g2in.next()
                S.op("sp", lambda e, y_=y_, csl=csl, tsl=tsl: e.dma_start(out=y_, in_=yT[csl, tsl]), writes=[y_r], dma=True)
                S.op("sp", lambda e, bv_=bv_, csl=csl, tsl=tsl: e.dma_start(out=bv_, in_=bvT[csl, tsl]), writes=[bv_r], dma=True)
                S.op("sp", lambda e, g2_=g2_, csl=csl, tsl=tsl: e.dma_start(out=g2_, in_=g2T[csl, tsl]), writes=[g2_r], dma=True)
                (d_, d_r), (sqd, sqd_r), (sd, sd_r), (rs, rs_r), (t_, t_r), (t2, t2_r) = [r.next() for r in W5]
                ps_m, ps_mr = nb5()
                S.op("pe", lambda e, ps_m=ps_m, y_=y_: e.matmul(ps_m, lhsT=b64, rhs=y_, start=True, stop=True), reads=[y_r, r_c5], writes=[ps_mr])
                S.op("dve", lambda e, d_=d_, y_=y_, ps_m=ps_m: e.tensor_tensor(out=d_, in0=y_, in1=ps_m, op=ALU.subtract), reads=[y_r, ps_mr], writes=[d_r])
                S.op("act", lambda e, d_=d_, sqd=sqd: e.activation(out=sqd, in_=d_, func=AF.Square), reads=[d_r], writes=[sqd_r])
                ps_v, ps_vr = nb5()
                S.op("pe", lambda e, ps_v=ps_v, sqd=sqd: e.matmul(ps_v, lhsT=b64, rhs=sqd, start=True, stop=True), reads=[sqd_r, r_c5], writes=[ps_vr])
                yield
                S.op("act", lambda e, ps_v=ps_v, sd=sd: e.activation(out=sd, in_=ps_v, func=AF.Ln, bias=64e-5), reads=[ps_vr], writes=[sd_r])
                S.op("act", lambda e, sd=sd, rs=rs: e.activation(out=rs, in_=sd, func=AF.Exp, scale=-0.5), reads=[sd_r], writes=[rs_r])
                S.op("dve", lambda e, t_=t_, d_=d_, rs=rs, ct=ct: e.scalar_tensor_tensor(out=t_, in0=d_, scalar=chv5[:, 5, ct:ct + 1], in1=rs, op0=ALU.mult, op1=ALU.mult), reads=[d_r, rs_r, r_c5], writes=[t_r])
                S.op("dve", lambda e, t2=t2, t_=t_, bv_=bv_, ct=ct: e.scalar_tensor_tensor(out=t2, in0=t_, scalar=chv5[:, 6, ct:ct + 1], in1=bv_, op0=ALU.add, op1=ALU.add), reads=[t_r, bv_r, r_c5], writes=[t2_r])
                o_, o_r = oo.next()
                S.op("pool", lambda e, o_=o_, t2=t2, g2_=g2_: e.tensor_tensor(out=o_, in0=t2, in1=g2_, op=ALU.mult), reads=[t2_r, g2_r], writes=[o_r])
                S.op("pool", lambda e, o_=o_, csl=csl, tsl=tsl: e.dma_start(out=orwT[csl, tsl], in_=o_), reads=[o_r], dma=True)

            prev5 = None
            for ct in range(8):
                for tb in range(8):
                    g5 = block5(ct, tb)
                    next(g5)
                    if prev5 is not None:
                        next(prev5, None)
                    prev5 = g5
            next(prev5, None)
            S.barrier()
            A.reset(base_off)

        cast_i = [0]

        def load_cast(stg, dst, src, np_=128):
            sb, sr = stg.next()
            S.op("sp", lambda e: e.dma_start(out=sb[0:np_, :], in_=src), writes=[sr], dma=True)
            eng = ("dve", "pool", "act")[cast_i[0] % 3]
            cast_i[0] += 1
            r = Res("wc")
            if eng == "act":
                S.op("act", lambda e: e.copy(out=dst, in_=sb[0:np_, :]), reads=[sr], writes=[r])
            else:
                S.op(eng, lambda e: e.tensor_copy(out=dst, in_=sb[0:np_, :]), reads=[sr], writes=[r])
            return r

        def rms_tile(src, src_r, gb, gb_r, dst, dst_r, add=None, add_r=None, rings=None):
            junk, junk_r, ssr_, rtr_, rsr_ = rings
            ss, ssres = ssr_.next()
            rt, rtres = rtr_.next()
            rs, rsres = rsr_.next()
            S.op("act", lambda e: e.activation(out=junk, in_=src, func=AF.Square, accum_out=ss), reads=[src_r], writes=[junk_r, ssres])
            S.op("act", lambda e: e.activation(out=rt, in_=ss, func=AF.Sqrt, scale=1.0 / D, bias=1e-6), reads=[ssres], writes=[rtres])
            S.op("dve", lambda e: e.reciprocal(out=rs, in_=rt), reads=[rtres], writes=[rsres])
            if add is None:
                S.op("dve", lambda e: e.scalar_tensor_tensor(out=dst, in0=src, scalar=rs[:, 0:1], in1=gb, op0=ALU.mult, op1=ALU.mult), reads=[src_r, rsres, gb_r], writes=[dst_r])
            else:
                S.op("dve", lambda e: e.scalar_tensor_tensor(out=src, in0=src, scalar=rs[:, 0:1], in1=gb, op0=ALU.mult, op1=ALU.mult), reads=[rsres, gb_r], writes=[src_r])
                S.op("pool", lambda e: e.tensor_tensor(out=dst, in0=src, in1=add, op=ALU.add), reads=[src_r, add_r], writes=[dst_r])

        if n_phase >= 6 and 6 not in skip:
            stg = Ring(A, 2, [D], F32, "stg")
            wab = A.alloc([4, D], BF16)
            wrb = A.alloc([8, D], BF16)
            wo = A.alloc([8, D], BF16)
            wres = []
            for h in range(4):
                wres.append(load_cast(stg, wab[0:64, h, :], wab_in[h * 64:(h + 1) * 64, :], 64))
            for kc in range(8):
                wres.append(load_cast(stg, wrb[:, kc, :], wrb_in[kc * 128:(kc + 1) * 128, :]))
            wres_o = []
            for kc in range(8):
                wres_o.append(load_cast(stg, wo[:, kc, :], wout_in[kc * 128:(kc + 1) * 128, :]))
            gpo = A.alloc([D], F32)
            r_gpo = Res("gpo")
            S.op("sp", lambda e: e.dma_start(out=gpo, in_=g_post.partition_broadcast(128)), writes=[r_gpo], dma=True)
            oat = Ring(A, 2, [4, 512], BF16, "oat")
            orw = Ring(A, 2, [8, 512], BF16, "orw")
            gtr = Ring(A, 3, [2, 512], BF16, "gt")
            mrg = Ring(A, 2, [8, 512], BF16, "mrg")
            m1r = Ring(A, 2, [512], F32, "m1")
            m2r = Ring(A, 2, [512], F32, "m2")
            m2t = Ring(A, 2, [D], F32, "m2t")
            xin = Ring(A, 2, [D], F32, "xin")
            x1o = Ring(A, 2, [D], F32, "x1o")
            junk5 = A.alloc([D], BF16)
            rings5 = (junk5, Res("junk5"), Ring(A, 2, [1], F32, "ss5"), Ring(A, 2, [1], F32, "rt5"), Ring(A, 2, [1], F32, "rs5"))
            pb6 = [0]

            def nb6():
                b = pb6[0]
                pb6[0] = (b + 1) % 8
                return banks[b], bank_res[b]

            gT4 = gT.rearrange("(b mc p) t -> p b mc t", b=2, p=128)
            for tb in range(8):
                tsl = slice(tb * 512, (tb + 1) * 512)
                oa, oa_r = oat.next()
                ow, ow_r = orw.next()
                mg, mg_r = mrg.next()
                S.op("sp", lambda e, oa=oa, tsl=tsl: e.dma_start(out=oa[0:64], in_=oattT.rearrange("(h p) t -> p h t", p=64)[:, :, tsl]), writes=[oa_r], dma=True)
                S.op("sp", lambda e, ow=ow, tsl=tsl: e.dma_start(out=ow, in_=orwT.rearrange("(c p) t -> p c t", p=128)[:, :, tsl]), writes=[ow_r], dma=True)
                for mc in range(8):
                    gt, gt_r = gtr.next()
                    S.op("sp", lambda e, gt=gt, mc=mc, tsl=tsl: e.dma_start(out=gt, in_=gT4[:, :, mc, tsl]), writes=[gt_r], dma=True)
                    ps_a, ps_ar = nb6()
                    ps_r, ps_rr = nb6()
                    for h in range(4):
                        S.op("pe", lambda e, ps_a=ps_a, h=h, mc=mc, oa=oa: e.matmul(ps_a, lhsT=wab[0:64, h, mc * 128:(mc + 1) * 128], rhs=oa[0:64, h, :], start=(h == 0), stop=(h == 3)), reads=[oa_r] + wres, writes=[ps_ar])
                    for ct in range(8):
                        S.op("pe", lambda e, ps_r=ps_r, ct=ct, mc=mc, ow=ow: e.matmul(ps_r, lhsT=wrb[:, ct, mc * 128:(mc + 1) * 128], rhs=ow[:, ct, :], start=(ct == 0), stop=(ct == 7)), reads=[ow_r] + wres, writes=[ps_rr])
                    m1, m1_r = m1r.next()
                    m2, m2_r = m2r.next()
                    S.op("dve", lambda e, m1=m1, ps_a=ps_a, gt=gt: e.tensor_tensor(out=m1, in0=ps_a, in1=gt[:, 0, :], op=ALU.mult), reads=[ps_ar, gt_r], writes=[m1_r])
                    S.op("dve", lambda e, m2=m2, ps_r=ps_r, gt=gt: e.tensor_tensor(out=m2, in0=ps_r, in1=gt[:, 1, :], op=ALU.mult), reads=[ps_rr, gt_r], writes=[m2_r])
                    S.op("pool", lambda e, mg=mg, mc=mc, m1=m1, m2=m2: e.tensor_tensor(out=mg[:, mc, :], in0=m1, in1=m2, op=ALU.add), reads=[m1_r, m2_r], writes=[mg_r])
                for tq in range(4):
                    tok0 = tb * 512 + tq * 128
                    mt, mt_r = m2t.next()
                    for nh in range(2):
                        ps, ps_r_ = nb6()
                        for mc in range(8):
                            S.op("pe", lambda e, ps=ps, mc=mc, tq=tq, nh=nh, mg=mg: e.matmul(ps, lhsT=mg[:, mc, tq * 128:(tq + 1) * 128], rhs=wo[:, mc, nh * 512:(nh + 1) * 512], start=(mc == 0), stop=(mc == 7)), reads=[mg_r] + wres_o, writes=[ps_r_])
                        S.op("act", lambda e, ps=ps, mt=mt, nh=nh: e.copy(out=mt[:, nh * 512:(nh + 1) * 512], in_=ps), reads=[ps_r_], writes=[mt_r])
                    xi, xi_r = xin.next()
                    S.op("sp", lambda e, xi=xi, tok0=tok0: e.dma_start(out=xi, in_=x[tok0:tok0 + 128, :]), writes=[xi_r], dma=True)
                    xo, xo_r = x1o.next()
                    rms_tile(mt, mt_r, gpo, r_gpo, xo, xo_r, add=xi, add_r=xi_r, rings=rings5)
                    S.op("pool", lambda e, xo=xo, tok0=tok0: e.dma_start(out=x1s[tok0:tok0 + 128, :], in_=xo), reads=[xo_r], dma=True)
            S.barrier()
            A.reset(base_off)

        if n_phase >= 7 and 7 not in skip:
            stg = Ring(A, 2, [D], F32, "stg6")
            w1b = A.alloc([8, 4096], BF16)
            w2b = A.alloc([32, D], BF16)
            wres = []
            for kc in range(8):
                for q in range(4):
                    wres.append(load_cast(stg, w1b[:, kc, q * 1024:(q + 1) * 1024], w1_in[kc * 128:(kc + 1) * 128, q * 1024:(q + 1) * 1024]))
            for fc in range(32):
                wres.append(load_cast(stg, w2b[:, fc, :], w2_in[fc * 128:(fc + 1) * 128, :]))
            gfa = A.alloc([D], F32)
            gfb = A.alloc([D], F32)
            r_gf = Res("gf")
            S.op("sp", lambda e: e.dma_start(out=gfa, in_=g_fpre.partition_broadcast(128)), writes=[r_gf], dma=True)
            S.op("sp", lambda e: e.dma_start(out=gfb, in_=g_fpost.partition_broadcast(128)), writes=[r_gf], dma=True)
            x1r = Ring(A, 4, [D], F32, "x1r")
            hb6 = Ring(A, 2, [D], BF16, "hb6")
            h2T = A.alloc([8, 256], BF16)
            r_h2T = Res("h2T")
            ffT = A.alloc([32, 256], BF16)
            r_ffT = Res("ffT")
            rl = Ring(A, 3, [256], BF16, "rl")
            ffo = Ring(A, 2, [D], F32, "ffo")
            junk6 = A.alloc([D], BF16)
            rings6 = (junk6, Res("junk6"), Ring(A, 2, [1], F32, "ss6"), Ring(A, 2, [1], F32, "rt6"), Ring(A, 2, [1], F32, "rs6"))
            pb7 = [0]

            def nb7():
                b = pb7[0]
                pb7[0] = (b + 1) % 8
                return banks[b], bank_res[b]

            def block6(tb):
                xts = []
                for tq in range(2):
                    tok0 = tb * 256 + tq * 128
                    xt_, xt_r = x1r.next()
                    xts.append((xt_, xt_r))
                    S.op("sp", lambda e, xt_=xt_, tok0=tok0: e.dma_start(out=xt_, in_=x1s[tok0:tok0 + 128, :]), writes=[xt_r], dma=True)
                    hb, hb_r = hb6.next()
                    rms_tile(xt_, xt_r, gfa, r_gf, hb, hb_r, rings=rings6)
                    bk7, bk_r = nb7()
                    bkv = bk7.bitcast(BF16).rearrange("p (a b) -> p a b", a=8)
                    for kc in range(8):
                        S.op("pe", lambda e, bkv=bkv, hb=hb, kc=kc: e.transpose(out=bkv[:, kc, :], in_=hb[:, kc * 128:(kc + 1) * 128], identity=ident_b), reads=[hb_r, r_ident], writes=[bk_r])
                    S.op("act", lambda e, bkv=bkv, tq=tq: e.copy(out=h2T[:, :, tq * 128:(tq + 1) * 128], in_=bkv), reads=[bk_r], writes=[r_h2T])
                yield
                for fc in range(32):
                    ps, ps_r_ = nb7()
                    for kc in range(8):
                        S.op("pe", lambda e, ps=ps, kc=kc, fc=fc: e.matmul(ps[:, 0:256], lhsT=w1b[:, kc, fc * 128:(fc + 1) * 128], rhs=h2T[:, kc, :], start=(kc == 0), stop=(kc == 7)), reads=[r_h2T] + wres[:32], writes=[ps_r_])
                    rl_, rl_r = rl.next()
                    S.op("act", lambda e, ps=ps, rl_=rl_: e.activation(out=rl_, in_=ps[:, 0:256], func=AF.Relu), reads=[ps_r_], writes=[rl_r])
                    S.op("pool", lambda e, rl_=rl_, fc=fc: e.tensor_tensor(out=ffT[:, fc, :], in0=rl_, in1=rl_, op=ALU.mult), reads=[rl_r], writes=[r_ffT])
                yield
                for tq in range(2):
                    tok0 = tb * 256 + tq * 128
                    fo_, fo_r = ffo.next()
                    for nh in range(2):
                        ps, ps_r_ = nb7()
                        for fc in range(32):
                            S.op("pe", lambda e, ps=ps, fc=fc, tq=tq, nh=nh: e.matmul(ps, lhsT=ffT[:, fc, tq * 128:(tq + 1) * 128], rhs=w2b[:, fc, nh * 512:(nh + 1) * 512], start=(fc == 0), stop=(fc == 31)), reads=[r_ffT] + wres[32:], writes=[ps_r_])
                        S.op("act", lambda e, ps=ps, fo_=fo_, nh=nh: e.copy(out=fo_[:, nh * 512:(nh + 1) * 512], in_=ps), reads=[ps_r_], writes=[fo_r])
                    xt_, xt_r = xts[tq]
                    rms_tile(fo_, fo_r, gfb, r_gf, xt_, xt_r, add=xt_, add_r=xt_r, rings=rings6)
                    S.op("pool", lambda e, xt_=xt_, tok0=tok0: e.dma_start(out=out[tok0:tok0 + 128, :], in_=xt_), reads=[xt_r], dma=True)

            g6 = {0: block6(0)}
            next(g6[0])
            for tb in range(16):
                cur = g6.pop(tb)
                next(cur)
                if tb + 1 < 16:
                    g6[tb + 1] = block6(tb + 1)
                    next(g6[tb + 1])
                next(cur, None)
            S.barrier()
            A.reset(base_off)

        S.barrier()
        S.emit()
    return nc


def host_prep(inp):
    f32 = np.float32
    c = {}
    c["ident"] = np.eye(128, dtype=f32)
    c["w_in"] = np.ascontiguousarray(inp["w_in"][0])
    for k in ("norm_mix_pre", "norm_mix_post", "norm_ffn_pre", "norm_ffn_post"):
        c[k] = np.ascontiguousarray(inp[k][0:1])
    c["mu_l"] = np.ascontiguousarray(inp["shift_mu"][0].reshape(26, 128).T)
    c["bgate_l"] = np.ascontiguousarray(inp["b_gate"][0].reshape(16, 128).T)
    dil = (1, 4, 16)
    j = np.arange(129)
    mbt = np.full((128, 12, 2, 128), NEG, dtype=f32)
    kk = np.arange(128)[:, None]
    qq = np.arange(128)[None, :]
    for g, d in enumerate(dil):
        dist = d * j
        d_f = np.maximum(dist, 1).astype(f32)
        large = 16 + (np.log(d_f / f32(16)) / f32(np.log(2048 / 16)) * f32(16)).astype(np.int32)
        large = np.minimum(large, 31)
        bucket = np.where(dist < 16, dist, large)
        for hh in range(4):
            head = g * 4 + hh
            bj = inp["rel_bias"][bucket, head]
            jd = qq - kk
            jp = qq + 128 - kk
            md = np.where(jd >= 0, bj[np.clip(jd, 0, 128)], f32(NEG))
            mp = np.where(jp <= 128, bj[np.clip(jp, 0, 128)], f32(NEG))
            mbt[:, head, 0, :] = mp
            mbt[:, head, 1, :] = md
    c["mbt"] = mbt
    pl = lambda v: np.ascontiguousarray(np.asarray(v, dtype=f32).reshape(8, 128).T)
    c["chv"] = np.ascontiguousarray(np.stack([pl(inp["w0"][0]), pl(inp["a0"][0]), pl(inp["k_k"][0]), pl(inp["k_a"][0]),
                                              pl(inp["r_k"][0].reshape(-1)), pl(inp["ln_x_w"][0]), pl(inp["ln_x_b"][0])], axis=1))
    bo = np.zeros((128, 128), f32)
    bo[0:64, 0:64] = 1.0
    bo[64:128, 64:128] = 1.0
    c["bones"] = bo
    c["bones64"] = bo * f32(1.0 / 64.0)
    mc = np.ones((128, 512), f32)
    mc[:, 0::64] = 0.0
    c["maskc"] = mc
    rr = np.arange(64)[:, None]
    cc = np.arange(64)[None, :]
    strict = (rr < cc).astype(f32)
    incl = (rr <= cc).astype(f32)
    c["maskA"] = np.ascontiguousarray(np.tile(np.concatenate([strict, incl], axis=1), (1, 4)))
    c["maskL"] = np.ascontiguousarray(np.tile((cc < rr).astype(f32), (1, 2)))
    c["wab"] = np.ascontiguousarray(inp["w_att_branch"][0])
    c["wrb"] = np.ascontiguousarray(inp["w_rwkv_branch"][0])
    c["wout"] = np.ascontiguousarray(inp["w_out"][0])
    c["w1"] = np.ascontiguousarray(inp["w_ffn1"][0])
    c["w2"] = np.ascontiguousarray(inp["w_ffn2"][0])
    c["ww2"] = np.ascontiguousarray(inp["w_w2"][0])
    c["wa2"] = np.ascontiguousarray(inp["w_a2"][0])
    c["wg2"] = np.ascontiguousarray(inp["w_g2"][0])
    return c


_NC_CACHE = {}


def kernel(**inputs):
    inp = {k: np.asarray(v) for k, v in inputs.items()}
    c = host_prep(inp)
    if "nc" not in _NC_CACHE:
        _NC_CACHE["nc"] = build()
    nc = _NC_CACHE["nc"]
    in_maps = []
    for b in range(8):
        m = dict(c)
        m["x"] = np.ascontiguousarray(inp["x"][b])
        in_maps.append(m)
    res = run_bass_kernel_spmd(nc, in_maps, core_ids=list(range(8)))
    return np.stack([r["out"] for r in res.results], axis=0).astype(np.float32)
```

```python
import contextlib
import numpy as np
import concourse.bass as bass
import concourse.mybir as mybir
from concourse.bass_utils import run_bass_kernel_spmd

F32 = mybir.dt.float32
BF16 = mybir.dt.bfloat16
U8 = mybir.dt.uint8
AF = mybir.ActivationFunctionType
ALU = mybir.AluOpType
AX = mybir.AxisListType

S_LEN = 4096
D = 1024
NEG = -30000.0


class Res:
    __slots__ = ("name", "w", "r")

    def __init__(self, name=""):
        self.name = name
        self.w = None
        self.r = {}


class Sched:
    COMPUTE = ("pe", "act", "dve", "pool")
    QUEUES = ("pe", "act", "dve", "pool", "sp")

    def __init__(self, nc, n_dma_sems=48, same_engine_sync=True):
        self.nc = nc
        self.ops = {e: [] for e in self.QUEUES}
        self.cnt = {e: 0 for e in self.COMPUTE}
        self.known = {e: {} for e in self.QUEUES}
        self.n_dma = n_dma_sems
        self.dval = [0] * n_dma_sems
        self.dnext = 0
        self.dq = {}
        self.same = same_engine_sync
        self.sems = {}
        self.nops = 0

    def _add_wait(self, eng, waits, tok):
        if tok is None:
            return
        sk, v = tok
        if sk == eng and (eng == "pe" or not self.same):
            return
        if self.known[eng].get(sk, 0) >= v:
            return
        self.known[eng][sk] = v
        waits[sk] = max(waits.get(sk, 0), v)

    def op(self, eng, fn, reads=(), writes=(), dma=False):
        waits = {}
        for r in reads:
            self._add_wait(eng, waits, r.w)
        for w in writes:
            self._add_wait(eng, waits, w.w)
            for sk, v in w.r.items():
                self._add_wait(eng, waits, (sk, v))
        if dma:
            half = self.n_dma // 2
            base = 0 if eng == "sp" else half
            self.dq[eng] = (self.dq.get(eng, -1) + 1) % half
            k = base + self.dq[eng]
            sk = ("d", k)
            if self.dval[k] > 0:
                self._add_wait(eng, waits, (sk, self.dval[k]))
            self.dval[k] += 16
            tok = (sk, self.dval[k])
            inc = (sk, 16)
        else:
            self.cnt[eng] += 1
            tok = (eng, self.cnt[eng])
            inc = (eng, 1)
        self.ops[eng].append((list(waits.items()), fn, inc))
        self.nops += 1
        for r in reads:
            sk, v = tok
            if r.r.get(sk, 0) < v:
                r.r[sk] = v
        for w in writes:
            w.w = tok
            w.r = {}
        return tok

    def barrier(self):
        if getattr(self, "pe_reset", None) is not None:
            self.pe_reset()
        toks = [(("d", k), self.dval[k]) for k in range(self.n_dma) if self.dval[k] > 0]
        toks += [(e, self.cnt[e]) for e in self.COMPUTE if self.cnt[e] > 0]
        for eng in self.QUEUES:
            waits = {}
            for sk, v in toks:
                if self.known[eng].get(sk, 0) >= v:
                    continue
                self.known[eng][sk] = v
                waits[sk] = v
            if waits:
                self.ops[eng].append((list(waits.items()), None, None))

    def emit(self):
        nc = self.nc
        with contextlib.ExitStack() as st:
            keys = list(self.COMPUTE) + [("d", k) for k in range(self.n_dma)]
            for k in keys:
                nm = k if isinstance(k, str) else "d%d" % k[1]
                self.sems[k] = st.enter_context(nc.semaphore("s_" + nm))
            block = st.enter_context(nc.Block())
            sems = self.sems

            import os as _os2
            attach = int(_os2.environ.get("DBG_ATTACH", 1))

            def replay(name, eng):
                for waits, fn, inc in self.ops[name]:
                    if fn is None or not attach or not waits:
                        for sk, v in waits:
                            eng.wait_ge(sems[sk], v)
                        if fn is None:
                            continue
                        ins = fn(eng)
                    else:
                        for sk, v in waits[:-1]:
                            eng.wait_ge(sems[sk], v)
                        ins = fn(eng)
                        ins._wait_ge(sems[waits[-1][0]], waits[-1][1])
                    ins.then_inc(sems[inc[0]], inc[1])

            @block.tensor
            def _(eng):
                replay("pe", eng)

            @block.scalar
            def _(eng):
                replay("act", eng)

            @block.vector
            def _(eng):
                replay("dve", eng)

            @block.gpsimd
            def _(eng):
                replay("pool", eng)

            @block.sync
            def _(eng):
                replay("sp", eng)


class Arena:
    def __init__(self, t, nbytes):
        self.t = t
        self.n = nbytes
        self.off = 0

    def reset(self, off=0):
        self.off = off

    def alloc(self, free_shape, dt):
        esz = 4 if dt == F32 else (2 if dt == BF16 else 1)
        n = esz
        for s in free_shape:
            n *= s
        off = (self.off + 63) // 64 * 64
        assert off + n <= self.n, ("SBUF arena overflow", off, n, self.n)
        self.off = off + n
        ap = self.t[:, off:off + n].bitcast(dt)
        if len(free_shape) == 2:
            ap = ap.rearrange("p (a b) -> p a b", a=free_shape[0])
        elif len(free_shape) == 3:
            ap = ap.rearrange("p (a b c) -> p a b c", a=free_shape[0], b=free_shape[1])
        elif len(free_shape) == 4:
            ap = ap.rearrange("p (a b c d) -> p a b c d", a=free_shape[0], b=free_shape[1], c=free_shape[2])
        return ap


class Ring:
    def __init__(self, arena, n, free_shape, dt, name="ring"):
        self.bufs = [arena.alloc(free_shape, dt) for _ in range(n)]
        self.res = [Res(name + str(i)) for i in range(n)]
        self.i = -1
        self.n = n

    def next(self):
        self.i = (self.i + 1) % self.n
        return self.bufs[self.i], self.res[self.i]


C_Q, C_K, C_V, C_RW, C_GATE, C_END = 0, 768, 1536, 2304, 5632, 7680
N_RW = 3328


def build(n_phase=99, dbg=False, skip=()):
    nc = bass.Bass("TRN2", target_bir_lowering=False)
    IN = lambda name, shape, dt=F32: nc.dram_tensor(name, shape, dt, kind="ExternalInput").ap()
    SCR = lambda name, shape, dt: nc.dram_tensor(name, shape, dt, kind=("ExternalOutput" if dbg else "Internal")).ap()
    x = IN("x", [S_LEN, D])
    w_in = IN("w_in", [D, C_END])
    ident_in = IN("ident", [128, 128])
    g_pre = IN("norm_mix_pre", [1, D])
    g_post = IN("norm_mix_post", [1, D])
    g_fpre = IN("norm_ffn_pre", [1, D])
    g_fpost = IN("norm_ffn_post", [1, D])
    mu_in = IN("mu_l", [128, 26])
    bg_in = IN("bgate_l", [128, 16])
    mbt_in = IN("mbt", [128, 12, 2, 128])
    out = nc.dram_tensor("out", [S_LEN, D], F32, kind="ExternalOutput").ap()

    qkT = SCR("qkT", [1536, S_LEN], BF16)
    vA = SCR("vA", [S_LEN, 768], BF16)
    frw = IN("frw_in", [N_RW, S_LEN]) if (dbg and 1 in skip) else SCR("frw", [N_RW, S_LEN], F32)
    gT = SCR("gT", [2048, S_LEN], BF16)
    oattT = SCR("oattT", [256, S_LEN], BF16)
    chv_in = IN("chv", [128, 7, 8])
    bones_in = IN("bones", [128, 128])
    bones64_in = IN("bones64", [128, 128])
    maskc_in = IN("maskc", [128, 512])
    maskA_in = IN("maskA", [64, 512])
    maskL_in = IN("maskL", [64, 128])
    ww2_in = IN("ww2", [64, D])
    wa2_in = IN("wa2", [64, D])
    wg2_in = IN("wg2", [128, D])
    wab_in = IN("wab", [256, D])
    wrb_in = IN("wrb", [D, D])
    wout_in = IN("wout", [D, D])
    w1_in = IN("w1", [D, 4096])
    w2_in = IN("w2", [4096, D])
    x1s = SCR("x1s", [S_LEN, D], F32)
    ARs = SCR("ARs", [64, 64, 8, 2, 128], BF16)
    BKs = SCR("BKs", [64, 64, 8, 2, 128], BF16)
    TKs = SCR("TKs", [64, 64, 8, 384], BF16)
    g2T = SCR("g2T", [D, S_LEN], BF16)
    bvT = SCR("bvT", [D, S_LEN], BF16)
    yT = SCR("yT", [D, S_LEN], F32)
    orwT = SCR("orwT", [D, S_LEN], BF16)

    with contextlib.ExitStack() as st:
        arena_t = st.enter_context(nc.sbuf_tensor("arena", [128, 204 * 1024], U8))[:, :]
        A = Arena(arena_t, 204 * 1024)
        banks = [st.enter_context(nc.psum_tensor("bank%d" % i, [128, 512], F32))[:, :] for i in range(8)]
        bank_res = [Res("bank%d" % i) for i in range(8)]
        S = Sched(nc)

        ident_f = A.alloc([128], F32)
        ident_b = A.alloc([128], BF16)
        r_ident = Res("ident")
        S.op("sp", lambda e: e.dma_start(out=ident_f, in_=ident_in), writes=[r_ident], dma=True)
        S.op("dve", lambda e: e.tensor_copy(out=ident_b, in_=ident_f), reads=[r_ident], writes=[r_ident])
        base_off = A.off

        def _pe_reset():
            S.op("pe", lambda e: e.matmul(banks[7][:, 0:128], lhsT=ident_b, rhs=ident_b, start=True, stop=True), reads=[r_ident, bank_res[7]], writes=[bank_res[7]])
        S.pe_reset = _pe_reset

        if n_phase >= 1 and 1 not in skip:
            hT = A.alloc([8, S_LEN], BF16)
            r_hT = [Res("hT%d" % i) for i in range(32)]
            gbc = A.alloc([D], F32)
            r_g = Res("gpre")
            S.op("sp", lambda e: e.dma_start(out=gbc, in_=g_pre.partition_broadcast(128)), writes=[r_g], dma=True)
            mu_t = A.alloc([26], F32)
            omm_t = A.alloc([26], F32)
            bg_t = A.alloc([16], F32)
            r_mu = Res("mu")
            S.op("sp", lambda e: e.dma_start(out=mu_t, in_=mu_in), writes=[r_mu], dma=True)
            S.op("sp", lambda e: e.dma_start(out=bg_t, in_=bg_in), writes=[r_mu], dma=True)
            S.op("dve", lambda e: e.tensor_scalar(out=omm_t, in0=mu_t, scalar1=-1.0, scalar2=1.0, op0=ALU.mult, op1=ALU.add), reads=[r_mu], writes=[r_mu])
            mark = A.off
            xt = Ring(A, 2, [D], F32, "xt")
            junk = A.alloc([D], BF16)
            r_junk = Res("junk")
            ssr = Ring(A, 2, [1], F32, "ss")
            rtr = Ring(A, 2, [1], F32, "rt")
            rsr = Ring(A, 2, [1], F32, "rstd")
            hbr = Ring(A, 2, [D], BF16, "hb")
            pb = 0
            for tt in range(32):
                xb, xr = xt.next()
                ss, ssres = ssr.next()
                rt, rtres = rtr.next()
                rs, rsres = rsr.next()
                hb, hbres = hbr.next()
                S.op("sp", lambda e, xb=xb, tt=tt: e.dma_start(out=xb, in_=x[tt * 128:(tt + 1) * 128, :]), writes=[xr], dma=True)
                S.op("act", lambda e, xb=xb, ss=ss: e.activation(out=junk, in_=xb, func=AF.Square, accum_out=ss), reads=[xr], writes=[r_junk, ssres])
                S.op("act", lambda e, ss=ss, rt=rt: e.activation(out=rt, in_=ss, func=AF.Sqrt, scale=1.0 / D, bias=1e-6), reads=[ssres], writes=[rtres])
                S.op("dve", lambda e, rt=rt, rs=rs: e.reciprocal(out=rs, in_=rt), reads=[rtres], writes=[rsres])
                S.op("dve", lambda e, xb=xb, rs=rs, hb=hb: e.scalar_tensor_tensor(out=hb, in0=xb, scalar=rs[:, 0:1], in1=gbc, op0=ALU.mult, op1=ALU.mult), reads=[xr, rsres, r_g], writes=[hbres])
                bk = banks[pb].bitcast(BF16).rearrange("p (a b) -> p a b", a=8)
                br = bank_res[pb]
                pb = (pb + 1) % 2
                for kc in range(8):
                    S.op("pe", lambda e, bk=bk, hb=hb, kc=kc: e.transpose(out=bk[:, kc, :], in_=hb[:, kc * 128:(kc + 1) * 128], identity=ident_b), reads=[hbres, r_ident], writes=[br])
                eng = "act" if tt % 2 == 0 else "dve"
                if eng == "act":
                    S.op("act", lambda e, bk=bk, tt=tt: e.copy(out=hT[:, :, tt * 128:(tt + 1) * 128], in_=bk), reads=[br], writes=[r_hT[tt]])
                else:
                    S.op("dve", lambda e, bk=bk, tt=tt: e.tensor_copy(out=hT[:, :, tt * 128:(tt + 1) * 128], in_=bk), reads=[br], writes=[r_hT[tt]])

            wst = Ring(A, 2, [8, 512], F32, "wst")
            wbf = Ring(A, 2, [8, 512], BF16, "wbf")
            praw = Ring(A, 3, [513], F32, "praw")
            tmpr = Ring(A, 2, [512], F32, "tmp")
            fo32 = Ring(A, 3, [512], F32, "fo32")
            fo16 = Ring(A, 3, [512], BF16, "fo16")
            w_view = w_in.rearrange("(kc p) c -> p kc c", p=128)
            pbk = [2]

            def nextbank():
                b = pbk[0]
                pbk[0] = 2 + (pbk[0] - 2 + 1) % 6
                return banks[b], bank_res[b]

            loaded = {}

            def load_w(cb):
                ws, wsr = wst.next()
                wb, wbr = wbf.next()
                S.op("sp", lambda e: e.dma_start(out=ws, in_=w_view[:, :, cb * 512:(cb + 1) * 512]), writes=[wsr], dma=True)
                S.op("pool", lambda e: e.tensor_copy(out=wb[:, 0:4, :], in_=ws[:, 0:4, :]), reads=[wsr], writes=[wbr])
                S.op("pool", lambda e: e.tensor_copy(out=wb[:, 4:8, :], in_=ws[:, 4:8, :]), reads=[wsr], writes=[wbr])
                loaded[cb] = (wb, wbr)

            NCB = 15
            load_w(0)
            for cb in range(NCB):
                if cb + 1 < NCB:
                    load_w(cb + 1)
                wb, wbr = loaded.pop(cb)
                vlo = max(C_V, cb * 512)
                vhi = min(C_RW, (cb + 1) * 512)
                if vlo < vhi:
                    n = vhi - vlo
                    lo = vlo - cb * 512
                    for tt in range(32):
                        ps, psr = nextbank()
                        for kc in range(8):
                            S.op("pe", lambda e, ps=ps, kc=kc, tt=tt, lo=lo, n=n, wb=wb: e.matmul(ps[:, 0:n], lhsT=hT[:, kc, tt * 128:(tt + 1) * 128], rhs=wb[:, kc, lo:lo + n], start=(kc == 0), stop=(kc == 7)), reads=[r_hT[tt], wbr], writes=[psr])
                        fo, fr_ = fo16.next()
                        S.op("act", lambda e, ps=ps, fo=fo, n=n: e.copy(out=fo[:, 0:n], in_=ps[:, 0:n]), reads=[psr], writes=[fr_])
                        S.op("pool", lambda e, fo=fo, n=n, tt=tt, vlo=vlo: e.dma_start(out=vA[tt * 128:(tt + 1) * 128, vlo - C_V:vlo - C_V + n], in_=fo[:, 0:n]), reads=[fr_], dma=True)
                for sb in range(4):
                    c0 = cb * 512 + sb * 128
                    if C_V <= c0 < C_RW:
                        continue
                    for tb in range(8):
                        ps, psr = nextbank()
                        for kc in range(8):
                            S.op("pe", lambda e, ps=ps, kc=kc, tb=tb, sb=sb, wb=wb: e.matmul(ps[:, :], lhsT=wb[:, kc, sb * 128:(sb + 1) * 128], rhs=hT[:, kc, tb * 512:(tb + 1) * 512], start=(kc == 0), stop=(kc == 7)), reads=[r_hT[4 * tb + i] for i in range(4)] + [wbr], writes=[psr])
                        if c0 < C_V:
                            fo, fr_ = fo16.next()
                            if tb % 2 == 0:
                                S.op("act", lambda e, ps=ps, fo=fo: e.copy(out=fo, in_=ps), reads=[psr], writes=[fr_])
                            else:
                                S.op("dve", lambda e, ps=ps, fo=fo: e.tensor_copy(out=fo, in_=ps), reads=[psr], writes=[fr_])
                            S.op("pool", lambda e, fo=fo, c0=c0, tb=tb: e.dma_start(out=qkT[c0:c0 + 128, tb * 512:(tb + 1) * 512], in_=fo), reads=[fr_], dma=True)
                        elif c0 >= C_GATE:
                            j = (c0 - C_GATE) // 128
                            fo, fr_ = fo16.next()
                            S.op("act", lambda e, ps=ps, fo=fo, j=j: e.activation(out=fo, in_=ps, func=AF.Sigmoid, bias=bg_t[:, j:j + 1]), reads=[psr, r_mu], writes=[fr_])
                            S.op("pool", lambda e, fo=fo, c0=c0, tb=tb: e.dma_start(out=gT[c0 - C_GATE:c0 - C_GATE + 128, tb * 512:(tb + 1) * 512], in_=fo), reads=[fr_], dma=True)
                        else:
                            j = (c0 - C_RW) // 128
                            if tb == 0:
                                pr, prr = praw.next()
                                S.op("pool", lambda e, pr=pr: e.memset(pr[:, 0:1], 0.0), writes=[prr])
                            else:
                                pr, prr = nxt
                            S.op("act", lambda e, ps=ps, pr=pr: e.copy(out=pr[:, 1:513], in_=ps), reads=[psr], writes=[prr])
                            if tb < 7:
                                nxt = praw.next()
                                S.op("pool", lambda e, pr=pr, np_=nxt[0]: e.tensor_copy(out=np_[:, 0:1], in_=pr[:, 512:513]), reads=[prr], writes=[nxt[1]])
                            tm, tmr = tmpr.next()
                            fo, fr_ = fo32.next()
                            S.op("dve", lambda e, pr=pr, tm=tm, j=j: e.tensor_scalar(out=tm, in0=pr[:, 0:512], scalar1=mu_t[:, j:j + 1], scalar2=None, op0=ALU.mult), reads=[prr, r_mu], writes=[tmr])
                            S.op("dve", lambda e, pr=pr, tm=tm, fo=fo, j=j: e.scalar_tensor_tensor(out=fo, in0=pr[:, 1:513], scalar=omm_t[:, j:j + 1], in1=tm, op0=ALU.mult, op1=ALU.add), reads=[prr, tmr, r_mu], writes=[fr_])
                            S.op("pool", lambda e, fo=fo, c0=c0, tb=tb: e.dma_start(out=frw[c0 - C_RW:c0 - C_RW + 128, tb * 512:(tb + 1) * 512], in_=fo), reads=[fr_], dma=True)
            S.barrier()
            A.reset(base_off)

        if n_phase >= 2 and 2 not in skip:
            mbt = A.alloc([12, 2, 128], F32)
            r_mbt = Res("mbt")
            S.op("sp", lambda e: e.dma_start(out=mbt, in_=mbt_in), writes=[r_mbt], dma=True)
            kq = Ring(A, 2, [2, S_LEN], BF16, "kq")
            vd = Ring(A, 2, [32, 128], BF16, "vd")
            acc = A.alloc([S_LEN], F32)
            r_acc = Res("acc")
            denlo = A.alloc([S_LEN], F32)
            r_den = Res("denlo")
            obf = A.alloc([S_LEN], BF16)
            r_obf = Res("obf")
            tmpS = Ring(A, 4, [128], F32, "tmpS")
            Ebr = Ring(A, 6, [128], BF16, "E")
            tmpO = Ring(A, 4, [128], F32, "tmpO")
            for i in range(2):
                S.op("pool", lambda e, b=vd.bufs[i]: e.memset(b[:, :, 64:128], 1.0), writes=[vd.res[i]])
            cnt2 = [0, 0]
            prev2 = [None]

            def tile2(g, d, head, kqb, kqr, vb, vr, ntile, r, mq):
                q0 = r + d * 128 * mq
                qsl = slice(q0, q0 + d * 127 + 1, d)
                kts = [mq - 1, mq] if mq > 0 else [mq]
                psO, psOr = banks[4 + cnt2[1] % 4], bank_res[4 + cnt2[1] % 4]
                cnt2[1] += 1
                Es = []
                for kt in kts:
                    k0 = r + d * 128 * kt
                    ksl = slice(k0, k0 + d * 127 + 1, d)
                    psS, psSr = banks[cnt2[0] % 4], bank_res[cnt2[0] % 4]
                    cnt2[0] += 1
                    S.op("pe", lambda e, psS=psS, kqb=kqb, ksl=ksl, qsl=qsl: e.matmul(psS[:, 0:128], lhsT=kqb[0:64, 0, ksl], rhs=kqb[0:64, 1, qsl], start=True, stop=True), reads=[kqr], writes=[psSr])
                    tm, tmr = tmpS.next()
                    which = 1 if kt == mq else 0
                    S.op("dve", lambda e, psS=psS, tm=tm, head=head, which=which: e.scalar_tensor_tensor(out=tm, in0=psS[:, 0:128], scalar=0.125, in1=mbt[:, head, which, :], op0=ALU.mult, op1=ALU.add), reads=[psSr, r_mbt], writes=[tmr])
                    Eb_, Er = Ebr.next()
                    S.op("act", lambda e, tm=tm, Eb_=Eb_: e.activation(out=Eb_, in_=tm, func=AF.Exp), reads=[tmr], writes=[Er])
                    Es.append((Eb_, Er, kt))
                yield
                for i, (Eb_, Er, kt) in enumerate(Es):
                    ti = r * ntile + kt
                    S.op("pe", lambda e, psO=psO, vb=vb, ti=ti, Eb_=Eb_, i=i, n=len(Es): e.matmul(psO[:, 0:128], lhsT=vb[:, ti, :], rhs=Eb_, start=(i == 0), stop=(i == n - 1)), reads=[vr, Er], writes=[psOr])
                if g == 0:
                    S.op("act", lambda e, psO=psO, qsl=qsl: e.copy(out=acc[:, qsl], in_=psO[:, 0:128]), reads=[psOr], writes=[r_acc])
                elif cnt2[1] % 2 == 0:
                    to_, to_r = tmpO.next()
                    S.op("act", lambda e, psO=psO, to_=to_: e.copy(out=to_, in_=psO[:, 0:128]), reads=[psOr], writes=[to_r])
                    S.op("pool", lambda e, to_=to_, qsl=qsl: e.tensor_tensor(out=acc[:, qsl], in0=to_, in1=acc[:, qsl], op=ALU.add), reads=[to_r, r_acc], writes=[r_acc])
                else:
                    S.op("dve", lambda e, psO=psO, qsl=qsl: e.tensor_tensor(out=acc[:, qsl], in0=psO[:, 0:128], in1=acc[:, qsl], op=ALU.add), reads=[psOr, r_acc], writes=[r_acc])

            for hh in range(4):
                for g, d in enumerate((1, 4, 16)):
                    head = g * 4 + hh
                    kqb, kqr = kq.next()
                    vb, vr = vd.next()
                    S.op("sp", lambda e, kqb=kqb, head=head: e.dma_start(out=kqb[0:64, 0, :], in_=qkT[768 + head * 64:768 + (head + 1) * 64, :]), writes=[kqr], dma=True)
                    S.op("sp", lambda e, kqb=kqb, head=head: e.dma_start(out=kqb[0:64, 1, :], in_=qkT[head * 64:(head + 1) * 64, :]), writes=[kqr], dma=True)
                    ntile = 32 // d
                    if d == 1:
                        for c4 in range(4):
                            S.op("sp", lambda e, vb=vb, head=head, c4=c4: e.dma_start(out=vb[:, c4 * 8:(c4 + 1) * 8, 0:64], in_=vA[c4 * 1024:(c4 + 1) * 1024, head * 64:(head + 1) * 64].rearrange("(mt p) c -> p mt c", p=128)), writes=[vr], dma=True)
                    else:
                        for r in range(d):
                            S.op("sp", lambda e, vb=vb, head=head, r=r, d=d, ntile=ntile: e.dma_start(out=vb[:, r * ntile:(r + 1) * ntile, 0:64], in_=vA[r:S_LEN:d, head * 64:(head + 1) * 64].rearrange("(mt p) c -> p mt c", p=128)), writes=[vr], dma=True)
                    for r in range(d):
                        for mq in range(ntile):
                            gnew = tile2(g, d, head, kqb, kqr, vb, vr, ntile, r, mq)
                            next(gnew)
                            if prev2[0] is not None:
                                next(prev2[0], None)
                            prev2[0] = gnew
                if prev2[0] is not None:
                    next(prev2[0], None)
                    prev2[0] = None
                S.op("sp", lambda e: e.dma_start(out=denlo[0:64, :], in_=acc[64:128, :]), reads=[r_acc], writes=[r_den], dma=True)
                S.op("act", lambda e: e.activation(out=denlo[0:64, :], in_=denlo[0:64, :], func=AF.Ln), reads=[r_den], writes=[r_den])
                S.op("act", lambda e: e.activation(out=denlo[0:64, :], in_=denlo[0:64, :], func=AF.Exp, scale=-1.0), reads=[r_den], writes=[r_den])
                S.op("dve", lambda e: e.tensor_tensor(out=obf[0:64, :], in0=acc[0:64, :], in1=denlo[0:64, :], op=ALU.mult), reads=[r_acc, r_den], writes=[r_obf])
                S.op("pool", lambda e, hh=hh: e.dma_start(out=oattT[hh * 64:(hh + 1) * 64, :], in_=obf[0:64, :]), reads=[r_obf], dma=True)
            S.barrier()
            A.reset(base_off)

        if n_phase >= 3 and 3 not in skip:
            C0 = 0.6065306597126334
            gam = A.alloc([8, 64], F32)
            r_gam = [Res("gam%d" % i) for i in range(8)]
            chv = A.alloc([7, 8], F32)
            omka = A.alloc([8], F32)
            r_chv = Res("chv")
            S.op("sp", lambda e: e.dma_start(out=chv, in_=chv_in), writes=[r_chv], dma=True)
            S.op("dve", lambda e: e.tensor_scalar(out=omka, in0=chv[:, 3, :], scalar1=-1.0, scalar2=1.0, op0=ALU.mult, op1=ALU.add), reads=[r_chv], writes=[r_chv])
            bones_f = A.alloc([128], F32)
            bones_b = A.alloc([128], BF16)
            r_bones = Res("bones")
            S.op("sp", lambda e: e.dma_start(out=bones_f, in_=bones_in), writes=[r_bones], dma=True)
            S.op("dve", lambda e: e.tensor_copy(out=bones_b, in_=bones_f), reads=[r_bones], writes=[r_bones])
            base3 = A.off
            maskc = A.alloc([512], F32)
            r_maskc = Res("maskc")
            S.op("sp", lambda e: e.dma_start(out=maskc, in_=maskc_in), writes=[r_maskc], dma=True)
            lst = Ring(A, 2, [S_LEN], F32, "lst")
            ww2 = A.alloc([D], BF16)
            wa2 = A.alloc([D], BF16)
            wg2 = A.alloc([D], BF16)
            tw = A.alloc([S_LEN], BF16)
            fab = A.alloc([S_LEN], BF16)
            sgb = A.alloc([S_LEN], BF16)
            r_lora = Res("lora")
            for (src, dst, np_, fn) in ((ww2_in, ww2, 64, None), (wa2_in, wa2, 64, None), (wg2_in, wg2, 128, None),
                                        (frw[3072:3136, :], tw, 64, AF.Tanh), (frw[3136:3200, :], fab, 64, AF.Copy), (frw[3200:3328, :], sgb, 128, AF.Sigmoid)):
                lb, lr = lst.next()
                n = src.shape[1]
                S.op("sp", lambda e, lb=lb, src=src, np_=np_, n=n: e.dma_start(out=lb[0:np_, 0:n], in_=src), writes=[lr], dma=True)
                if fn is None or fn == AF.Copy:
                    S.op("act", lambda e, lb=lb, dst=dst, np_=np_, n=n: e.copy(out=dst[0:np_, 0:n], in_=lb[0:np_, 0:n]), reads=[lr], writes=[r_lora])
                else:
                    S.op("act", lambda e, lb=lb, dst=dst, np_=np_, n=n, fn=fn: e.activation(out=dst[0:np_, 0:n], in_=lb[0:np_, 0:n], func=fn), reads=[lr], writes=[r_lora])
            NR = 2
            fin = [Ring(A, 3, [512], F32, "fin%d" % i) for i in range(3)]
            T32 = [Ring(A, NR, [512], F32, "t32_%d" % i) for i in range(16)]
            T16 = [Ring(A, NR, [512], BF16, "t16_%d" % i) for i in range(6)]
            ARo_r = Ring(A, NR, [8, 2, 64], BF16, "ARo")
            BKo_r = Ring(A, NR, [8, 2, 64], BF16, "BKo")
            TKo_r = Ring(A, NR, [8, 3, 128], BF16, "TKo")
            pbk3 = [0]

            def nb3():
                b = pbk3[0]
                pbk3[0] = (b + 1) % 8
                return banks[b], bank_res[b]

            def v3(ap):
                return ap.rearrange("p (c t) -> p c t", c=8)

            def p3_load(ct, tb):
                tsl = slice(tb * 512, (tb + 1) * 512)
                bufs = [f.next() for f in fin]
                for i, (buf, res) in enumerate(bufs):
                    S.op("sp", lambda e, buf=buf, i=i, tsl=tsl, ct=ct: e.dma_start(out=buf, in_=frw[i * 1024 + ct * 128:i * 1024 + (ct + 1) * 128, tsl]), writes=[res], dma=True)
                return bufs

            blocks3 = [(ct, tb) for ct in range(8) for tb in range(8)]

            def block3(bi, ct, tb, bufs):
                tsl = slice(tb * 512, (tb + 1) * 512)
                csl = slice(ct * 128, (ct + 1) * 128)
                (fr_, fr_r), (fk_, fk_r), (fv_, fv_r) = bufs
                t = [r.next() for r in T32]
                h = [r.next() for r in T16]
                (sgm, sgm_r), (a_, a_r), (k1, k1_r), (nr, nr_r), (kk, kk_r), (t1, t1_r), (k2, k2_r), (b_, b_r) = t[0:8]
                (cs, cs_r), (tmp1, tmp1_r), (tmp2, tmp2_r), (einc, einc_r), (eneg, eneg_r), (eexc, eexc_r), (eend, eend_r), (rn, rn_r) = t[8:16]
                (g_o, g_or), (sq, sq_r), (rk, rk_r), (bv, bv_r), (khat, khat_r), (bhat, bhat_r) = h
                ps_w, ps_wr = nb3()
                ps_a, ps_ar = nb3()
                ps_g, ps_gr = nb3()
                S.op("pe", lambda e, ps_w=ps_w, csl=csl, tsl=tsl: e.matmul(ps_w, lhsT=ww2[0:64, csl], rhs=tw[0:64, tsl], start=True, stop=True), reads=[r_lora], writes=[ps_wr])
                S.op("pe", lambda e, ps_a=ps_a, csl=csl, tsl=tsl: e.matmul(ps_a, lhsT=wa2[0:64, csl], rhs=fab[0:64, tsl], start=True, stop=True), reads=[r_lora], writes=[ps_ar])
                S.op("pe", lambda e, ps_g=ps_g, csl=csl, tsl=tsl: e.matmul(ps_g, lhsT=wg2[:, csl], rhs=sgb[:, tsl], start=True, stop=True), reads=[r_lora], writes=[ps_gr])
                S.op("act", lambda e, ps_w=ps_w, sgm=sgm, ct=ct: e.activation(out=sgm, in_=ps_w, func=AF.Sigmoid, bias=chv[:, 0, ct:ct + 1]), reads=[ps_wr, r_chv], writes=[sgm_r])
                S.op("act", lambda e, ps_a=ps_a, a_=a_, ct=ct: e.activation(out=a_, in_=ps_a, func=AF.Sigmoid, bias=chv[:, 1, ct:ct + 1]), reads=[ps_ar, r_chv], writes=[a_r])
                S.op("dve", lambda e, ps_g=ps_g, g_o=g_o: e.tensor_copy(out=g_o, in_=ps_g), reads=[ps_gr], writes=[g_or])
                S.op("sp", lambda e, g_o=g_o, csl=csl, tsl=tsl: e.dma_start(out=g2T[csl, tsl], in_=g_o), reads=[g_or], dma=True)
                S.op("dve", lambda e, k1=k1, fk_=fk_, ct=ct: e.tensor_scalar(out=k1, in0=fk_, scalar1=chv[:, 2, ct:ct + 1], scalar2=None, op0=ALU.mult), reads=[fk_r, r_chv], writes=[k1_r])
                S.op("act", lambda e, k1=k1, sq=sq: e.activation(out=sq, in_=k1, func=AF.Square), reads=[k1_r], writes=[sq_r])
                ps_n, ps_nr = nb3()
                S.op("pe", lambda e, ps_n=ps_n, sq=sq: e.matmul(ps_n, lhsT=bones_b, rhs=sq, start=True, stop=True), reads=[sq_r, r_bones], writes=[ps_nr])
                yield
                S.op("act", lambda e, ps_n=ps_n, nr=nr: e.activation(out=nr, in_=ps_n, func=AF.Ln, scale=float(2.0 ** 40)), reads=[ps_nr], writes=[nr_r])
                S.op("act", lambda e, nr=nr, rn=rn: e.activation(out=rn, in_=nr, func=AF.Exp, scale=-0.5, bias=13.862943611198906), reads=[nr_r], writes=[rn_r])
                S.op("dve", lambda e, kk=kk, k1=k1, rn=rn: e.scalar_tensor_tensor(out=kk, in0=rn, scalar=1e12, in1=k1, op0=ALU.min, op1=ALU.mult), reads=[k1_r, rn_r], writes=[kk_r])
                S.op("dve", lambda e, t1=t1, a_=a_, ct=ct: e.tensor_scalar(out=t1, in0=a_, scalar1=chv[:, 3, ct:ct + 1], scalar2=omka[:, ct:ct + 1], op0=ALU.mult, op1=ALU.add), reads=[a_r, r_chv], writes=[t1_r])
                S.op("dve", lambda e, k2=k2, fk_=fk_, t1=t1: e.tensor_tensor(out=k2, in0=fk_, in1=t1, op=ALU.mult), reads=[fk_r, t1_r], writes=[k2_r])
                S.op("pool", lambda e, b_=b_, kk=kk, a_=a_: e.tensor_tensor(out=b_, in0=kk, in1=a_, op=ALU.mult), reads=[kk_r, a_r], writes=[b_r])
                S.op("dve", lambda e, rk=rk, fr_=fr_, k2=k2, ct=ct: e.scalar_tensor_tensor(out=rk, in0=fr_, scalar=chv[:, 4, ct:ct + 1], in1=k2, op0=ALU.mult, op1=ALU.mult), reads=[fr_r, k2_r, r_chv], writes=[rk_r])
                ps_b, ps_br = nb3()
                S.op("pe", lambda e, ps_b=ps_b, rk=rk: e.matmul(ps_b, lhsT=bones_b, rhs=rk, start=True, stop=True), reads=[rk_r, r_bones], writes=[ps_br])
                S.op("dve", lambda e, ps_b=ps_b, bv=bv, fv_=fv_: e.tensor_tensor(out=bv, in0=ps_b, in1=fv_, op=ALU.mult), reads=[ps_br, fv_r], writes=[bv_r])
                S.op("sp", lambda e, bv=bv, csl=csl, tsl=tsl: e.dma_start(out=bvT[csl, tsl], in_=bv), reads=[bv_r], dma=True)
                yield
                S.op("dve", lambda e, cs=cs, sgm=sgm: e.tensor_tensor_scan(out=cs, data0=maskc, data1=sgm, initial=0.0, op0=ALU.mult, op1=ALU.add), reads=[sgm_r, r_maskc], writes=[cs_r])
                S.op("pool", lambda e, tmp1=tmp1, cs=cs, sgm=sgm: e.tensor_tensor(out=tmp1, in0=cs, in1=sgm, op=ALU.subtract), reads=[cs_r, sgm_r], writes=[tmp1_r])
                S.op("act", lambda e, einc=einc, cs=cs: e.activation(out=einc, in_=cs, func=AF.Exp, scale=-C0), reads=[cs_r], writes=[einc_r])
                S.op("act", lambda e, eneg=eneg, cs=cs: e.activation(out=eneg, in_=cs, func=AF.Exp, scale=C0), reads=[cs_r], writes=[eneg_r])
                S.op("act", lambda e, eexc=eexc, tmp1=tmp1: e.activation(out=eexc, in_=tmp1, func=AF.Exp, scale=-C0), reads=[tmp1_r], writes=[eexc_r])
                yield
                ARo, ARo_res = ARo_r.next()
                BKo, BKo_res = BKo_r.next()
                TKo, TKo_res = TKo_r.next()
                S.op("dve", lambda e, ARo=ARo, fr_=fr_, einc=einc: e.tensor_tensor(out=ARo[:, :, 1, :], in0=v3(fr_), in1=v3(einc), op=ALU.mult), reads=[fr_r, einc_r], writes=[ARo_res])
                S.op("dve", lambda e, ARo=ARo, kk=kk, eexc=eexc: e.scalar_tensor_tensor(out=ARo[:, :, 0, :], in0=v3(kk), scalar=-1.0, in1=v3(eexc), op0=ALU.mult, op1=ALU.mult), reads=[kk_r, eexc_r], writes=[ARo_res])
                S.op("pool", lambda e, BKo=BKo, b_=b_, eneg=eneg: e.tensor_tensor(out=BKo[:, :, 0, :], in0=v3(b_), in1=v3(eneg), op=ALU.mult), reads=[b_r, eneg_r], writes=[BKo_res])
                S.op("pool", lambda e, BKo=BKo, k2=k2, eneg=eneg: e.tensor_tensor(out=BKo[:, :, 1, :], in0=v3(k2), in1=v3(eneg), op=ALU.mult), reads=[k2_r, eneg_r], writes=[BKo_res])
                S.op("dve", lambda e, khat=khat, BKo=BKo, einc=einc: e.tensor_tensor(out=v3(khat), in0=BKo[:, :, 1, :], in1=v3(einc)[:, :, 63:64].to_broadcast([128, 8, 64]), op=ALU.mult), reads=[BKo_res, einc_r], writes=[khat_r])
                S.op("pool", lambda e, bhat=bhat, BKo=BKo, einc=einc: e.tensor_tensor(out=v3(bhat), in0=BKo[:, :, 0, :], in1=v3(einc)[:, :, 63:64].to_broadcast([128, 8, 64]), op=ALU.mult), reads=[BKo_res, einc_r], writes=[bhat_r])
                S.op("pool", lambda e, sq=sq, fv_=fv_: e.tensor_copy(out=sq, in_=fv_), reads=[fv_r], writes=[sq_r])
                S.op("pool", lambda e, einc=einc, ct=ct, tb=tb: e.tensor_copy(out=gam[:, ct, tb * 8:(tb + 1) * 8], in_=v3(einc)[:, :, 63]), reads=[einc_r], writes=[r_gam[ct]])
                yield
                for j, (src, src_r) in enumerate(((khat, khat_r), (bhat, bhat_r), (sq, sq_r))):
                    psT, psT_r = nb3()
                    pv = psT.bitcast(BF16).rearrange("p (c x) -> p c x", c=8)
                    for ch in range(8):
                        S.op("pe", lambda e, pv=pv, src=src, ch=ch: e.transpose(out=pv[0:64, ch, :], in_=src[:, ch * 64:(ch + 1) * 64], identity=ident_b), reads=[src_r, r_ident], writes=[psT_r])
                    if j <= 1:
                        S.op("dve", lambda e, pv=pv, TKo=TKo, j=j: e.tensor_copy(out=TKo[0:64, :, j, :], in_=pv[0:64, :, :]), reads=[psT_r], writes=[TKo_res])
                    else:
                        S.op("act", lambda e, pv=pv, TKo=TKo, j=j: e.copy(out=TKo[0:64, :, j, :], in_=pv[0:64, :, :]), reads=[psT_r], writes=[TKo_res])
                for hd in range(2):
                    S.op("sp", lambda e, ARo=ARo, ct=ct, tb=tb, hd=hd: e.dma_start(out=ARs[tb * 8:(tb + 1) * 8, :, ct, hd, :].rearrange("c p x -> p c x"), in_=ARo[hd * 64:(hd + 1) * 64].rearrange("p c a x -> p c (a x)")), reads=[ARo_res], dma=True)
                    S.op("sp", lambda e, BKo=BKo, ct=ct, tb=tb, hd=hd: e.dma_start(out=BKs[tb * 8:(tb + 1) * 8, :, ct, hd, :].rearrange("c p x -> p c x"), in_=BKo[hd * 64:(hd + 1) * 64].rearrange("p c a x -> p c (a x)")), reads=[BKo_res], dma=True)
                S.op("sp", lambda e, TKo=TKo, ct=ct, tb=tb: e.dma_start(out=TKs[tb * 8:(tb + 1) * 8, :, ct, :].rearrange("c p x -> p c x"), in_=TKo[0:64].rearrange("p c a x -> p c (a x)")), reads=[TKo_res], dma=True)

            n3 = len(blocks3)
            pre3 = {0: p3_load(*blocks3[0]), 1: p3_load(*blocks3[1])}
            g3 = {0: block3(0, *blocks3[0], pre3.pop(0))}
            next(g3[0]); next(g3[0]); next(g3[0])
            for bi in range(n3):
                if bi + 2 < n3:
                    pre3[bi + 2] = p3_load(*blocks3[bi + 2])
                nx = None
                if bi + 1 < n3:
                    nx = g3[bi + 1] = block3(bi + 1, *blocks3[bi + 1], pre3.pop(bi + 1))
                cur = g3.pop(bi)
                if nx is not None:
                    next(nx)
                next(cur)
                if nx is not None:
                    next(nx)
                next(cur, None)
                if nx is not None:
                    next(nx)
            S.barrier()
            A.reset(base3)

        if n_phase >= 4 and 4 not in skip:
            if 3 in skip:
                gam = A.alloc([8, 64], F32)
                r_gam = [Res('g') for _ in range(8)]
                base3 = A.off
            S.op('pe', lambda e: e.matmul(banks[7][:, 0:128], lhsT=ident_b, rhs=ident_b, start=True, stop=True), reads=[r_ident], writes=[bank_res[7]])
            maskA = A.alloc([512], F32)
            maskL = A.alloc([128], F32)
            r_mk = Res("masks")
            S.op("sp", lambda e: e.dma_start(out=maskA[0:64, :], in_=maskA_in), writes=[r_mk], dma=True)
            S.op("sp", lambda e: e.dma_start(out=maskL[0:64, :], in_=maskL_in), writes=[r_mk], dma=True)
            ident2 = A.alloc([2, 64], BF16)
            r_id2 = Res("ident2")
            for hd in range(2):
                S.op("pool", lambda e, hd=hd: e.tensor_copy(out=ident2[0:64, hd, :], in_=ident_b[0:64, 0:64]), reads=[r_ident], writes=[r_id2])
            gam2 = A.alloc([8, 2, 64], F32)
            r_gam2 = Res("gam2")
            S.op("sp", lambda e: e.dma_start(out=gam2[0:64, :, 0, :], in_=gam[0:64, :, :]), reads=r_gam, writes=[r_gam2], dma=True)
            S.op("sp", lambda e: e.dma_start(out=gam2[0:64, :, 1, :], in_=gam[64:128, :, :]), reads=r_gam, writes=[r_gam2], dma=True)
            ST = A.alloc([8, 2, 64], F32)
            STb = A.alloc([8, 2, 64], BF16)
            r_ST = [Res("ST%d" % i) for i in range(8)]
            r_STb = [Res("STb%d" % i) for i in range(8)]
            S.op("pool", lambda e: e.memset(ST, 0.0), writes=r_ST)
            S.op("pool", lambda e: e.memset(STb, 0.0), writes=r_STb)
            NCH = 3
            ARc = Ring(A, NCH, [8, 2, 2, 64], BF16, "ARc")
            BKc = Ring(A, NCH, [8, 2, 2, 64], BF16, "BKc")
            TKc = Ring(A, NCH, [8, 3, 128], BF16, "TKc")
            NW = 16
            AAr = Ring(A, NW, [2, 2, 128], BF16, "AA")
            NTr = Ring(A, NW, [2, 64], BF16, "NT")
            PPr = Ring(A, NW, [2, 2, 64], BF16, "PP")
            Tr = Ring(A, 2 * NW, [2, 64], BF16, "T")
            XTr = Ring(A, NW, [2, 64], BF16, "XT")
            UTr = Ring(A, NW, [2, 64], BF16, "UT")
            Yb = Ring(A, 2, [8, 2, 512], F32, "Yb")

            class PRing:
                def __init__(self, specs):
                    self.aps = [banks[b][:, lo:lo + n] for (b, lo, n) in specs]
                    self.res = [bank_res[b] for (b, lo, n) in specs]
                    self.i = -1

                def next(self):
                    self.i = (self.i + 1) % len(self.aps)
                    return self.aps[self.i], self.res[self.i]

            pA = PRing([(b, 0, 512) for b in (0, 1, 2, 3)])
            pA2 = PRing([(b, 0, 128) for b in (4, 5, 6, 7)])
            pP = PRing([(b, 0, 256) for b in (0, 1, 2, 3)])
            pT = PRing([(b, 0, 128) for b in (4, 5, 6, 7)])
            pX = PRing([(b, 0, 128) for b in (0, 1)])
            pU = PRing([(b, 0, 128) for b in (2, 3)])
            pY = PRing([(b, 0, 128) for b in (4, 5)])
            pS = PRing([(b, 0, 128) for b in (6, 7)])

            def v22(ap):
                return ap.rearrange("p (h a x) -> p h a x", h=2, a=2)

            def v2(ap, n):
                return ap.rearrange("p (h x) -> p h x", h=2)

            chunk_bufs = {}

            def load_chunk(c):
                (ar, ar_r), (bk, bk_r), (tk, tk_r) = ARc.next(), BKc.next(), TKc.next()
                S.op("sp", lambda e: e.dma_start(out=ar[0:64].rearrange("p c h a x -> p (c h a x)"), in_=ARs[c].rearrange("p c h x -> p (c h x)")), writes=[ar_r], dma=True)
                S.op("sp", lambda e: e.dma_start(out=bk[0:64].rearrange("p c h a x -> p (c h a x)"), in_=BKs[c].rearrange("p c h x -> p (c h x)")), writes=[bk_r], dma=True)
                S.op("sp", lambda e: e.dma_start(out=tk[0:64].rearrange("p c a x -> p (c a x)"), in_=TKs[c].rearrange("p c x -> p (c x)")), writes=[tk_r], dma=True)
                chunk_bufs[c] = (ar, ar_r, bk, bk_r, tk, tk_r)

            load_chunk(0)
            load_chunk(1)
            import os as _os

            def do_chunk(c, ar, ar_r, bk, bk_r, tk, tk_r, yb, yb_r):
                W = []
                if int(_os.environ.get('DBG_STEP', 9)) < 1:
                    return
                for ct in range(8):
                    psA, psA_r = pA.next()
                    psA2, psA2_r = pA2.next()
                    a4 = v22(psA)
                    a2 = v2(psA2, 64)
                    for hd in range(2):
                        p0 = hd * 64
                        S.op("pe", lambda e, a4=a4, hd=hd, p0=p0, ct=ct: e.matmul(a4[0:64, hd, 0, :], lhsT=bk[0:64, ct, hd, 0, :], rhs=ar[0:64, ct, hd, :, :].rearrange("p a x -> p (a x)"), start=True, stop=True), reads=[bk_r, ar_r], writes=[psA_r])
                        S.op("pe", lambda e, a4=a4, hd=hd, p0=p0, ct=ct: e.matmul(a4[0:64, hd, 1, :], lhsT=bk[0:64, ct, hd, 1, :], rhs=ar[0:64, ct, hd, :, :].rearrange("p a x -> p (a x)"), start=True, stop=True), reads=[bk_r, ar_r], writes=[psA_r])
                        S.op("pe", lambda e, a2=a2, hd=hd, p0=p0, ct=ct: e.matmul(a2[0:64, hd, :], lhsT=ar[0:64, ct, hd, 0, :], rhs=bk[0:64, ct, hd, 0, :], start=True, stop=True), reads=[bk_r, ar_r], writes=[psA2_r])
                    AA, AA_r = AAr.next()
                    NT, NT_r = NTr.next()
                    T0, T0_r = Tr.next()
                    _sub = int(_os.environ.get('DBG_SUB', 9))
                    if _sub >= 2:
                      S.op("dve", lambda e, psA=psA, AA=AA: e.tensor_tensor(out=AA[0:64].rearrange("p h a x -> p (h a x)"), in0=psA[0:64, :], in1=maskA[0:64, :], op=ALU.mult), reads=[r_mk], writes=[AA_r, psA_r])
                    if _sub >= 3:
                      S.op("dve", lambda e, psA2=psA2, NT=NT: e.tensor_tensor(out=NT[0:64].rearrange("p h x -> p (h x)"), in0=psA2[0:64, :], in1=maskL[0:64, :], op=ALU.mult), reads=[r_mk], writes=[NT_r, psA2_r])
                    if _sub >= 4:
                      S.op("pool", lambda e, AA=AA, T0=T0: e.tensor_tensor(out=T0[0:64], in0=AA[0:64, :, 0, 0:64], in1=ident2[0:64], op=ALU.add), reads=[AA_r, r_id2], writes=[T0_r])
                    if int(_os.environ.get('DBG_DELAY', 0)):
                        S.op("dve", lambda e, AA=AA: e.tensor_scalar(out=AA[0:64].rearrange("p h a x -> p (h a x)"), in0=AA[0:64].rearrange("p h a x -> p (h a x)"), scalar1=1.0, scalar2=None, op0=ALU.mult), reads=[r_mk], writes=[AA_r])
                        S.op("dve", lambda e, NT=NT: e.tensor_scalar(out=NT[0:64].rearrange("p h x -> p (h x)"), in0=NT[0:64].rearrange("p h x -> p (h x)"), scalar1=1.0, scalar2=None, op0=ALU.mult), reads=[r_mk], writes=[NT_r])
                    if int(_os.environ.get('DBG_CONST', 0)):
                        S.op("dve", lambda e, AA=AA: e.tensor_scalar(out=AA[0:64].rearrange("p h a x -> p (h a x)"), in0=maskA[0:64, :], scalar1=0.01, scalar2=None, op0=ALU.mult), reads=[r_mk], writes=[AA_r])
                        S.op("dve", lambda e, NT=NT: e.tensor_scalar(out=NT[0:64].rearrange("p h x -> p (h x)"), in0=maskL[0:64, :], scalar1=0.01, scalar2=None, op0=ALU.mult), reads=[r_mk], writes=[NT_r])
                    W.append(dict(AA=AA, AA_r=AA_r, P=(lambda AA=AA: AA[0:64, :, 0, 0:64]), PT=(lambda NT=NT: NT[0:64]), P_r=AA_r, PT_r=NT_r, T=T0, T_r=T0_r))
                yield
                _stp = int(_os.environ.get('DBG_STEP', 9))
                if _stp < 2:
                    return
                for lvl in range(int(_os.environ.get('DBG_LVL', 5))):
                    for ct in range(8):
                        w = W[ct]
                        psP, psP_r = pP.next()
                        p4 = psP.rearrange("p (h a x) -> p h a x", h=2, a=2)
                        Pp, PTp = w["P"](), w["PT"]()
                        for hd in range(2):
                            if lvl < 4:
                                S.op("pe", lambda e, p4=p4, hd=hd, Pp=Pp, PTp=PTp: e.matmul(p4[0:64, hd, 0, :], lhsT=PTp[:, hd, :], rhs=Pp[:, hd, :], start=True, stop=True), reads=[w["P_r"], w["PT_r"]], writes=[psP_r])
                            S.op("pe", lambda e, p4=p4, hd=hd, Pp=Pp, PTp=PTp: e.matmul(p4[0:64, hd, 1, :], lhsT=Pp[:, hd, :], rhs=PTp[:, hd, :], start=True, stop=True), reads=[w["P_r"], w["PT_r"]], writes=[psP_r])
                        PP, PP_r = PPr.next()
                        if int(_os.environ.get('DBG_NOCOPY', 0)):
                            pass
                        elif lvl < 4:
                            S.op("act", lambda e, psP=psP, PP=PP: e.copy(out=PP[0:64].rearrange("p h a x -> p (h a x)"), in_=psP[0:64, :]), reads=[], writes=[PP_r, psP_r])
                        else:
                            S.op("act", lambda e, p4=p4, PP=PP: e.copy(out=PP[0:64, :, 1, :], in_=p4[0:64, :, 1, :]), reads=[], writes=[PP_r, psP_r])
                        w["P"] = (lambda PP=PP: PP[0:64, :, 0, :])
                        w["PT"] = (lambda PP=PP: PP[0:64, :, 1, :])
                        w["P_r"] = PP_r
                        w["PT_r"] = PP_r
                    for ct in range(8 if int(_os.environ.get('DBG_T', 1)) else 0):
                        w = W[ct]
                        psT, psT_r = pT.next()
                        t2 = v2(psT, 64)
                        PTn = w["PT"]()
                        Tp = w["T"]
                        for hd in range(2):
                            S.op("pe", lambda e, t2=t2, hd=hd, PTn=PTn, Tp=Tp: e.matmul(t2[0:64, hd, :], lhsT=PTn[:, hd, :], rhs=Tp[0:64, hd, :], start=True, stop=True), reads=[w["PT_r"], w["T_r"]], writes=[psT_r])
                        Tn, Tn_r = Tr.next()
                        S.op("dve", lambda e, psT=psT, Tn=Tn, Tp=Tp: e.tensor_tensor(out=Tn[0:64].rearrange("p h x -> p (h x)"), in0=psT[0:64, :], in1=Tp[0:64].rearrange("p h x -> p (h x)"), op=ALU.add), reads=[w["T_r"]], writes=[Tn_r, psT_r])
                        w["T"] = Tn
                        w["T_r"] = Tn_r
                    yield
                if _stp < 3:
                    return
                for ct in range(8):
                    w = W[ct]
                    psX, psX_r = pX.next()
                    x2 = v2(psX, 64)
                    AA = w["AA"]
                    for hd in range(2):
                        p0 = hd * 64
                        S.op("pe", lambda e, x2=x2, hd=hd, p0=p0, ct=ct: e.matmul(x2[0:64, hd, :], lhsT=ar[0:64, ct, hd, 0, :], rhs=STb[0:64, ct, hd, :], start=True, stop=False), reads=[ar_r, r_STb[ct]], writes=[psX_r])
                        S.op("pe", lambda e, x2=x2, hd=hd, AA=AA, ct=ct: e.matmul(x2[0:64, hd, :], lhsT=AA[0:64, hd, 1, 0:64], rhs=tk[0:64, ct, 2, hd * 64:(hd + 1) * 64], start=False, stop=True), reads=[w["AA_r"], tk_r], writes=[psX_r])
                    XT, XT_r = XTr.next()
                    S.op("act", lambda e, psX=psX, XT=XT: e.copy(out=XT[0:64].rearrange("p h x -> p (h x)"), in_=psX[0:64, :]), reads=[], writes=[XT_r, psX_r])
                    w["XT"], w["XT_r"] = XT, XT_r
                yield
                if _stp < 4:
                    return
                for ct in range(8):
                    w = W[ct]
                    psU, psU_r = pU.next()
                    u2 = v2(psU, 64)
                    Tf, XT = w["T"], w["XT"]
                    for hd in range(2):
                        S.op("pe", lambda e, u2=u2, hd=hd, Tf=Tf, XT=XT: e.matmul(u2[0:64, hd, :], lhsT=Tf[0:64, hd, :], rhs=XT[0:64, hd, :], start=True, stop=True), reads=[w["T_r"], w["XT_r"]], writes=[psU_r])
                    UT, UT_r = UTr.next()
                    S.op("act", lambda e, psU=psU, UT=UT: e.copy(out=UT[0:64].rearrange("p h x -> p (h x)"), in_=psU[0:64, :]), reads=[], writes=[UT_r, psU_r])
                    w["UT"], w["UT_r"] = UT, UT_r
                yield
                if _stp < 5:
                    return
                for ct in range(8):
                    w = W[ct]
                    psY_, psY_r = pY.next()
                    psY = v2(psY_, 64)
                    AA, UT = w["AA"], w["UT"]
                    for hd in range(2):
                        p0 = hd * 64
                        S.op("pe", lambda e, psY=psY, p0=p0, ct=ct, hd=hd: e.matmul(psY[0:64, hd, :], lhsT=STb[0:64, ct, hd, :], rhs=ar[0:64, ct, hd, 1, :], start=True, stop=False), reads=[r_STb[ct], ar_r], writes=[psY_r])
                        S.op("pe", lambda e, psY=psY, p0=p0, ct=ct, hd=hd, AA=AA: e.matmul(psY[0:64, hd, :], lhsT=tk[0:64, ct, 2, hd * 64:(hd + 1) * 64], rhs=AA[0:64, hd, 1, 64:128], start=False, stop=False), reads=[tk_r, w["AA_r"]], writes=[psY_r])
                        S.op("pe", lambda e, psY=psY, p0=p0, hd=hd, AA=AA, UT=UT: e.matmul(psY[0:64, hd, :], lhsT=UT[0:64, hd, :], rhs=AA[0:64, hd, 0, 64:128], start=False, stop=True), reads=[w["UT_r"], w["AA_r"]], writes=[psY_r])
                    S.op("act", lambda e, psY=psY, yb=yb, ct=ct, c=c: e.copy(out=yb[0:64, ct, :, (c % 8) * 64:(c % 8 + 1) * 64], in_=psY[0:64]), reads=[], writes=[yb_r, psY_r])
                    psS_, psS_r = pS.next()
                    psS = v2(psS_, 64)
                    for hd in range(2):
                        p0 = hd * 64
                        S.op("pe", lambda e, psS=psS, p0=p0, ct=ct, hd=hd: e.matmul(psS[0:64, hd, :], lhsT=tk[0:64, ct, 0, hd * 64:(hd + 1) * 64], rhs=tk[0:64, ct, 2, hd * 64:(hd + 1) * 64], start=True, stop=False), reads=[tk_r], writes=[psS_r])
                        S.op("pe", lambda e, psS=psS, p0=p0, ct=ct, hd=hd, UT=UT: e.matmul(psS[0:64, hd, :], lhsT=tk[0:64, ct, 1, hd * 64:(hd + 1) * 64], rhs=UT[0:64, hd, :], start=False, stop=True), reads=[tk_r, w["UT_r"]], writes=[psS_r])
                    for hd in range(2):
                        S.op("dve", lambda e, psS=psS, ct=ct, c=c, hd=hd: e.scalar_tensor_tensor(out=ST[0:64, ct, hd, :], in0=ST[0:64, ct, hd, :], scalar=gam2[0:64, ct, hd, c:c + 1], in1=psS[0:64, hd, :], op0=ALU.mult, op1=ALU.add), reads=[r_gam2], writes=[r_ST[ct], psS_r])
                    S.op("pool", lambda e, ct=ct: e.tensor_copy(out=STb[0:64, ct, :, :], in_=ST[0:64, ct, :, :]), reads=[r_ST[ct]], writes=[r_STb[ct]])
                if c % 8 == 7:
                    tb = c // 8
                    S.op("pool", lambda e, yb=yb, tb=tb: e.dma_start(out=yT.rearrange("(ct hd p) t -> p ct hd t", hd=2, p=64)[:, :, :, tb * 512:(tb + 1) * 512], in_=yb[0:64]), reads=[yb_r], dma=True)

            nch = int(_os.environ.get('DBG_NCH', 64))
            ybs = {}

            def mk(c):
                if c % 8 == 0:
                    ybs[c // 8] = Yb.next()
                yb, yb_r = ybs[c // 8]
                return do_chunk(c, *chunk_bufs.pop(c), yb, yb_r)

            g4 = {0: mk(0)}
            for _ in range(6):
                next(g4[0])
            for c in range(nch):
                if c + 2 < 64:
                    load_chunk(c + 2)
                nx = None
                if c + 1 < nch:
                    nx = g4[c + 1] = mk(c + 1)
                cur = g4.pop(c)
                if nx is not None:
                    next(nx)
                next(cur)
                if nx is not None:
                    next(nx)
                next(cur)
                if nx is not None:
                    next(nx)
                next(cur, None)
                if nx is not None:
                    next(nx); next(nx); next(nx)
            S.barrier()
            A.reset(base_off)

        if n_phase >= 5 and 5 not in skip:
            chv5 = A.alloc([7, 8], F32)
            b64 = A.alloc([128], F32)
            r_c5 = Res("c5")
            S.op("sp", lambda e: e.dma_start(out=chv5, in_=chv_in), writes=[r_c5], dma=True)
            S.op("sp", lambda e: e.dma_start(out=b64, in_=bones64_in), writes=[r_c5], dma=True)
            yin = Ring(A, 2, [512], F32, "yin")
            bvin = Ring(A, 2, [512], BF16, "bvin")
            g2in = Ring(A, 2, [512], BF16, "g2in")
            W5 = [Ring(A, 2, [512], F32, "w5_%d" % i) for i in range(6)]
            oo = Ring(A, 2, [512], BF16, "oo")
            pb5 = [0]

            def nb5():
                b = pb5[0]
                pb5[0] = (b + 1) % 8
                return banks[b], bank_res[b]

            def block5(ct, tb):
                tsl = slice(tb * 512, (tb + 1) * 512)
                csl = slice(ct * 128, (ct + 1) * 128)
                y_, y_r = yin.next()
                bv_, bv_r = bvin.next()
                g2_, g2_r = g2in.next()
                S.op("sp", lambda e, y_=y_, csl=csl, tsl=tsl: e.dma_start(out=y_, in_=yT[csl, tsl]), writes=[y_r], dma=True)
                S.op("sp", lambda e, bv_=bv_, csl=csl, tsl=tsl: e.dma_start(out=bv_, in_=bvT[csl, tsl]), writes=[bv_r], dma=True)
                S.op("sp", lambda e, g2_=g2_, csl=csl, tsl=tsl: e.dma_start(out=g2_, in_=g2T[csl, tsl]), writes=[g2_r], dma=True)
                (d_, d_r), (sqd, sqd_r), (sd, sd_r), (rs, rs_r), (t_, t_r), (t2, t2_r) = [r.next() for r in W5]
                ps_m, ps_mr = nb5()
                S.op("pe", lambda e, ps_m=ps_m, y_=y_: e.matmul(ps_m, lhsT=b64, rhs=y_, start=True, stop=True), reads=[y_r, r_c5], writes=[ps_mr])
                S.op("dve", lambda e, d_=d_, y_=y_, ps_m=ps_m: e.tensor_tensor(out=d_, in0=y_, in1=ps_m, op=ALU.subtract), reads=[y_r, ps_mr], writes=[d_r])
                S.op("act", lambda e, d_=d_, sqd=sqd: e.activation(out=sqd, in_=d_, func=AF.Square), reads=[d_r], writes=[sqd_r])
                ps_v, ps_vr = nb5()
                S.op("pe", lambda e, ps_v=ps_v, sqd=sqd: e.matmul(ps_v, lhsT=b64, rhs=sqd, start=True, stop=True), reads=[sqd_r, r_c5], writes=[ps_vr])
                yield
                S.op("act", lambda e, ps_v=ps_v, sd=sd: e.activation(out=sd, in_=ps_v, func=AF.Ln, bias=64e-5), reads=[ps_vr], writes=[sd_r])
                S.op("act", lambda e, sd=sd, rs=rs: e.activation(out=rs, in_=sd, func=AF.Exp, scale=-0.5), reads=[sd_r], writes=[rs_r])
                S.op("dve", lambda e, t_=t_, d_=d_, rs=rs, ct=ct: e.scalar_tensor_tensor(out=t_, in0=d_, scalar=chv5[:, 5, ct:ct + 1], in1=rs, op0=ALU.mult, op1=ALU.mult), reads=[d_r, rs_r, r_c5], writes=[t_r])
                S.op("dve", lambda e, t2=t2, t_=t_, bv_=bv_, ct=ct: e.scalar_tensor_tensor(out=t2, in0=t_, scalar=chv5[:, 6, ct:ct + 1], in1=bv_, op0=ALU.add, op1=ALU.add), reads=[t_r, bv_r, r_c5], writes=[t2_r])
                o_, o_r = oo.next()
                S.op("pool", lambda e, o_=o_, t2=t2, g2_=g2_: e.tensor_tensor(out=o_, in0=t2, in1=g2_, op=ALU.mult), reads=[t2_r, g2_r], writes=[o_r])
                S.op("pool", lambda e, o_=o_, csl=csl, tsl=tsl: e.dma_start(out=orwT[csl, tsl], in_=o_), reads=[o_r], dma=True)

            prev5 = None
            for ct in range(8):
                for tb in range(8):
                    g5 = block5(ct, tb)
                    next(g5)
                    if prev5 is not None:
                        next(prev5, None)
                    prev5 = g5
            next(prev5, None)
            S.barrier()
            A.reset(base_off)

        cast_i = [0]

        def load_cast(stg, dst, src, np_=128):
            sb, sr = stg.next()
            S.op("sp", lambda e: e.dma_start(out=sb[0:np_, :], in_=src), writes=[sr], dma=True)
            eng = ("dve", "pool", "act")[cast_i[0] % 3]
            cast_i[0] += 1
            r = Res("wc")
            if eng == "act":
                S.op("act", lambda e: e.copy(out=dst, in_=sb[0:np_, :]), reads=[sr], writes=[r])
            else:
                S.op(eng, lambda e: e.tensor_copy(out=dst, in_=sb[0:np_, :]), reads=[sr], writes=[r])
            return r

        def rms_tile(src, src_r, gb, gb_r, dst, dst_r, add=None, add_r=None, rings=None):
            junk, junk_r, ssr_, rtr_, rsr_ = rings
            ss, ssres = ssr_.next()
            rt, rtres = rtr_.next()
            rs, rsres = rsr_.next()
            S.op("act", lambda e: e.activation(out=junk, in_=src, func=AF.Square, accum_out=ss), reads=[src_r], writes=[junk_r, ssres])
            S.op("act", lambda e: e.activation(out=rt, in_=ss, func=AF.Sqrt, scale=1.0 / D, bias=1e-6), reads=[ssres], writes=[rtres])
            S.op("dve", lambda e: e.reciprocal(out=rs, in_=rt), reads=[rtres], writes=[rsres])
            if add is None:
                S.op("dve", lambda e: e.scalar_tensor_tensor(out=dst, in0=src, scalar=rs[:, 0:1], in1=gb, op0=ALU.mult, op1=ALU.mult), reads=[src_r, rsres, gb_r], writes=[dst_r])
            else:
                S.op("dve", lambda e: e.scalar_tensor_tensor(out=src, in0=src, scalar=rs[:, 0:1], in1=gb, op0=ALU.mult, op1=ALU.mult), reads=[rsres, gb_r], writes=[src_r])
                S.op("pool", lambda e: e.tensor_tensor(out=dst, in0=src, in1=add, op=ALU.add), reads=[src_r, add_r], writes=[dst_r])

        if n_phase >= 6 and 6 not in skip:
            stg = Ring(A, 2, [D], F32, "stg")
            wab = A.alloc([4, D], BF16)
            wrb = A.alloc([8, D], BF16)
            wo = A.alloc([8, D], BF16)
            wres = []
            for h in range(4):
                wres.append(load_cast(stg, wab[0:64, h, :], wab_in[h * 64:(h + 1) * 64, :], 64))
            for kc in range(8):
                wres.append(load_cast(stg, wrb[:, kc, :], wrb_in[kc * 128:(kc + 1) * 128, :]))
            wres_o = []
            for kc in range(8):
                wres_o.append(load_cast(stg, wo[:, kc, :], wout_in[kc * 128:(kc + 1) * 128, :]))
            gpo = A.alloc([D], F32)
            r_gpo = Res("gpo")
            S.op("sp", lambda e: e.dma_start(out=gpo, in_=g_post.partition_broadcast(128)), writes=[r_gpo], dma=True)
            oat = Ring(A, 2, [4, 512], BF16, "oat")
            orw = Ring(A, 2, [8, 512], BF16, "orw")
            gtr = Ring(A, 3, [2, 512], BF16, "gt")
            mrg = Ring(A, 2, [8, 512], BF16, "mrg")
            m1r = Ring(A, 2, [512], F32, "m1")
            m2r = Ring(A, 2, [512], F32, "m2")
            m2t = Ring(A, 2, [D], F32, "m2t")
            xin = Ring(A, 2, [D], F32, "xin")
            x1o = Ring(A, 2, [D], F32, "x1o")
            junk5 = A.alloc([D], BF16)
            rings5 = (junk5, Res("junk5"), Ring(A, 2, [1], F32, "ss5"), Ring(A, 2, [1], F32, "rt5"), Ring(A, 2, [1], F32, "rs5"))
            pb6 = [0]

            def nb6():
                b = pb6[0]
                pb6[0] = (b + 1) % 8
                return banks[b], bank_res[b]

            gT4 = gT.rearrange("(b mc p) t -> p b mc t", b=2, p=128)
            for tb in range(8):
                tsl = slice(tb * 512, (tb + 1) * 512)
                oa, oa_r = oat.next()
                ow, ow_r = orw.next()
                mg, mg_r = mrg.next()
                S.op("sp", lambda e, oa=oa, tsl=tsl: e.dma_start(out=oa[0:64], in_=oattT.rearrange("(h p) t -> p h t", p=64)[:, :, tsl]), writes=[oa_r], dma=True)
                S.op("sp", lambda e, ow=ow, tsl=tsl: e.dma_start(out=ow, in_=orwT.rearrange("(c p) t -> p c t", p=128)[:, :, tsl]), writes=[ow_r], dma=True)
                for mc in range(8):
                    gt, gt_r = gtr.next()
                    S.op("sp", lambda e, gt=gt, mc=mc, tsl=tsl: e.dma_start(out=gt, in_=gT4[:, :, mc, tsl]), writes=[gt_r], dma=True)
                    ps_a, ps_ar = nb6()
                    ps_r, ps_rr = nb6()
                    for h in range(4):
                        S.op("pe", lambda e, ps_a=ps_a, h=h, mc=mc, oa=oa: e.matmul(ps_a, lhsT=wab[0:64, h, mc * 128:(mc + 1) * 128], rhs=oa[0:64, h, :], start=(h == 0), stop=(h == 3)), reads=[oa_r] + wres, writes=[ps_ar])
                    for ct in range(8):
                        S.op("pe", lambda e, ps_r=ps_r, ct=ct, mc=mc, ow=ow: e.matmul(ps_r, lhsT=wrb[:, ct, mc * 128:(mc + 1) * 128], rhs=ow[:, ct, :], start=(ct == 0), stop=(ct == 7)), reads=[ow_r] + wres, writes=[ps_rr])
                    m1, m1_r = m1r.next()
                    m2, m2_r = m2r.next()
                    S.op("dve", lambda e, m1=m1, ps_a=ps_a, gt=gt: e.tensor_tensor(out=m1, in0=ps_a, in1=gt[:, 0, :], op=ALU.mult), reads=[ps_ar, gt_r], writes=[m1_r])
                    S.op("dve", lambda e, m2=m2, ps_r=ps_r, gt=gt: e.tensor_tensor(out=m2, in0=ps_r, in1=gt[:, 1, :], op=ALU.mult), reads=[ps_rr, gt_r], writes=[m2_r])
                    S.op("pool", lambda e, mg=mg, mc=mc, m1=m1, m2=m2: e.tensor_tensor(out=mg[:, mc, :], in0=m1, in1=m2, op=ALU.add), reads=[m1_r, m2_r], writes=[mg_r])
                for tq in range(4):
                    tok0 = tb * 512 + tq * 128
                    mt, mt_r = m2t.next()
                    for nh in range(2):
                        ps, ps_r_ = nb6()
                        for mc in range(8):
                            S.op("pe", lambda e, ps=ps, mc=mc, tq=tq, nh=nh, mg=mg: e.matmul(ps, lhsT=mg[:, mc, tq * 128:(tq + 1) * 128], rhs=wo[:, mc, nh * 512:(nh + 1) * 512], start=(mc == 0), stop=(mc == 7)), reads=[mg_r] + wres_o, writes=[ps_r_])
                        S.op("act", lambda e, ps=ps, mt=mt, nh=nh: e.copy(out=mt[:, nh * 512:(nh + 1) * 512], in_=ps), reads=[ps_r_], writes=[mt_r])
                    xi, xi_r = xin.next()
                    S.op("sp", lambda e, xi=xi, tok0=tok0: e.dma_start(out=xi, in_=x[tok0:tok0 + 128, :]), writes=[xi_r], dma=True)
                    xo, xo_r = x1o.next()
                    rms_tile(mt, mt_r, gpo, r_gpo, xo, xo_r, add=xi, add_r=xi_r, rings=rings5)
                    S.op("pool", lambda e, xo=xo, tok0=tok0: e.dma_start(out=x1s[tok0:tok0 + 128, :], in_=xo), reads=[xo_r], dma=True)
            S.barrier()
            A.reset(base_off)

        if n_phase >= 7 and 7 not in skip:
            stg = Ring(A, 2, [D], F32, "stg6")
            w1b = A.alloc([8, 4096], BF16)
            w2b = A.alloc([32, D], BF16)
            wres = []
            for kc in range(8):
                for q in range(4):
                    wres.append(load_cast(stg, w1b[:, kc, q * 1024:(q + 1) * 1024], w1_in[kc * 128:(kc + 1) * 128, q * 1024:(q + 1) * 1024]))
            for fc in range(32):
                wres.append(load_cast(stg, w2b[:, fc, :], w2_in[fc * 128:(fc + 1) * 128, :]))
            gfa = A.alloc([D], F32)
            gfb = A.alloc([D], F32)
            r_gf = Res("gf")
            S.op("sp", lambda e: e.dma_start(out=gfa, in_=g_fpre.partition_broadcast(128)), writes=[r_gf], dma=True)
            S.op("sp", lambda e: e.dma_start(out=gfb, in_=g_fpost.partition_broadcast(128)), writes=[r_gf], dma=True)
            x1r = Ring(A, 4, [D], F32, "x1r")
            hb6 = Ring(A, 2, [D], BF16, "hb6")
            h2T = A.alloc([8, 256], BF16)
            r_h2T = Res("h2T")
            ffT = A.alloc([32, 256], BF16)
            r_ffT = Res("ffT")
            rl = Ring(A, 3, [256], BF16, "rl")
            ffo = Ring(A, 2, [D], F32, "ffo")
            junk6 = A.alloc([D], BF16)
            rings6 = (junk6, Res("junk6"), Ring(A, 2, [1], F32, "ss6"), Ring(A, 2, [1], F32, "rt6"), Ring(A, 2, [1], F32, "rs6"))
            pb7 = [0]

            def nb7():
                b = pb7[0]
                pb7[0] = (b + 1) % 8
                return banks[b], bank_res[b]

            def block6(tb):
                xts = []
                for tq in range(2):
                    tok0 = tb * 256 + tq * 128
                    xt_, xt_r = x1r.next()
                    xts.append((xt_, xt_r))
                    S.op("sp", lambda e, xt_=xt_, tok0=tok0: e.dma_start(out=xt_, in_=x1s[tok0:tok0 + 128, :]), writes=[xt_r], dma=True)
                    hb, hb_r = hb6.next()
                    rms_tile(xt_, xt_r, gfa, r_gf, hb, hb_r, rings=rings6)
                    bk7, bk_r = nb7()
                    bkv = bk7.bitcast(BF16).rearrange("p (a b) -> p a b", a=8)
                    for kc in range(8):
                        S.op("pe", lambda e, bkv=bkv, hb=hb, kc=kc: e.transpose(out=bkv[:, kc, :], in_=hb[:, kc * 128:(kc + 1) * 128], identity=ident_b), reads=[hb_r, r_ident], writes=[bk_r])
                    S.op("act", lambda e, bkv=bkv, tq=tq: e.copy(out=h2T[:, :, tq * 128:(tq + 1) * 128], in_=bkv), reads=[bk_r], writes=[r_h2T])
                yield
                for fc in range(32):
                    ps, ps_r_ = nb7()
                    for kc in range(8):
                        S.op("pe", lambda e, ps=ps, kc=kc, fc=fc: e.matmul(ps[:, 0:256], lhsT=w1b[:, kc, fc * 128:(fc + 1) * 128], rhs=h2T[:, kc, :], start=(kc == 0), stop=(kc == 7)), reads=[r_h2T] + wres[:32], writes=[ps_r_])
                    rl_, rl_r = rl.next()
                    S.op("act", lambda e, ps=ps, rl_=rl_: e.activation(out=rl_, in_=ps[:, 0:256], func=AF.Relu), reads=[ps_r_], writes=[rl_r])
                    S.op("pool", lambda e, rl_=rl_, fc=fc: e.tensor_tensor(out=ffT[:, fc, :], in0=rl_, in1=rl_, op=ALU.mult), reads=[rl_r], writes=[r_ffT])
                yield
                for tq in range(2):
                    tok0 = tb * 256 + tq * 128
                    fo_, fo_r = ffo.next()
                    for nh in range(2):
                        ps, ps_r_ = nb7()
                        for fc in range(32):
                            S.op("pe", lambda e, ps=ps, fc=fc, tq=tq, nh=nh: e.matmul(ps, lhsT=ffT[:, fc, tq * 128:(tq + 1) * 128], rhs=w2b[:, fc, nh * 512:(nh + 1) * 512], start=(fc == 0), stop=(fc == 31)), reads=[r_ffT] + wres[32:], writes=[ps_r_])
                        S.op("act", lambda e, ps=ps, fo_=fo_, nh=nh: e.copy(out=fo_[:, nh * 512:(nh + 1) * 512], in_=ps), reads=[ps_r_], writes=[fo_r])
                    xt_, xt_r = xts[tq]
                    rms_tile(fo_, fo_r, gfb, r_gf, xt_, xt_r, add=xt_, add_r=xt_r, rings=rings6)
                    S.op("pool", lambda e, xt_=xt_, tok0=tok0: e.dma_start(out=out[tok0:tok0 + 128, :], in_=xt_), reads=[xt_r], dma=True)

            g6 = {0: block6(0)}
            next(g6[0])
            for tb in range(16):
                cur = g6.pop(tb)
                next(cur)
                if tb + 1 < 16:
                    g6[tb + 1] = block6(tb + 1)
                    next(g6[tb + 1])
                next(cur, None)
            S.barrier()
            A.reset(base_off)

        S.barrier()
        S.emit()
    return nc


def host_prep(inp):
    f32 = np.float32
    c = {}
    c["ident"] = np.eye(128, dtype=f32)
    c["w_in"] = np.ascontiguousarray(inp["w_in"][0])
    for k in ("norm_mix_pre", "norm_mix_post", "norm_ffn_pre", "norm_ffn_post"):
        c[k] = np.ascontiguousarray(inp[k][0:1])
    c["mu_l"] = np.ascontiguousarray(inp["shift_mu"][0].reshape(26, 128).T)
    c["bgate_l"] = np.ascontiguousarray(inp["b_gate"][0].reshape(16, 128).T)
    dil = (1, 4, 16)
    j = np.arange(129)
    mbt = np.full((128, 12, 2, 128), NEG, dtype=f32)
    kk = np.arange(128)[:, None]
    qq = np.arange(128)[None, :]
    for g, d in enumerate(dil):
        dist = d * j
        d_f = np.maximum(dist, 1).astype(f32)
        large = 16 + (np.log(d_f / f32(16)) / f32(np.log(2048 / 16)) * f32(16)).astype(np.int32)
        large = np.minimum(large, 31)
        bucket = np.where(dist < 16, dist, large)
        for hh in range(4):
            head = g * 4 + hh
            bj = inp["rel_bias"][bucket, head]
            jd = qq - kk
            jp = qq + 128 - kk
            md = np.where(jd >= 0, bj[np.clip(jd, 0, 128)], f32(NEG))
            mp = np.where(jp <= 128, bj[np.clip(jp, 0, 128)], f32(NEG))
            mbt[:, head, 0, :] = mp
            mbt[:, head, 1, :] = md
    c["mbt"] = mbt
    pl = lambda v: np.ascontiguousarray(np.asarray(v, dtype=f32).reshape(8, 128).T)
    c["chv"] = np.ascontiguousarray(np.stack([pl(inp["w0"][0]), pl(inp["a0"][0]), pl(inp["k_k"][0]), pl(inp["k_a"][0]),
                                              pl(inp["r_k"][0].reshape(-1)), pl(inp["ln_x_w"][0]), pl(inp["ln_x_b"][0])], axis=1))
    bo = np.zeros((128, 128), f32)
    bo[0:64, 0:64] = 1.0
    bo[64:128, 64:128] = 1.0
    c["bones"] = bo
    c["bones64"] = bo * f32(1.0 / 64.0)
    mc = np.ones((128, 512), f32)
    mc[:, 0::64] = 0.0
    c["maskc"] = mc
    rr = np.arange(64)[:, None]
    cc = np.arange(64)[None, :]
    strict = (rr < cc).astype(f32)
    incl = (rr <= cc).astype(f32)
    c["maskA"] = np.ascontiguousarray(np.tile(np.concatenate([strict, incl], axis=1), (1, 4)))
    c["maskL"] = np.ascontiguousarray(np.tile((cc < rr).astype(f32), (1, 2)))
    c["wab"] = np.ascontiguousarray(inp["w_att_branch"][0])
    c["wrb"] = np.ascontiguousarray(inp["w_rwkv_branch"][0])
    c["wout"] = np.ascontiguousarray(inp["w_out"][0])
    c["w1"] = np.ascontiguousarray(inp["w_ffn1"][0])
    c["w2"] = np.ascontiguousarray(inp["w_ffn2"][0])
    c["ww2"] = np.ascontiguousarray(inp["w_w2"][0])
    c["wa2"] = np.ascontiguousarray(inp["w_a2"][0])
    c["wg2"] = np.ascontiguousarray(inp["w_g2"][0])
    return c


_NC_CACHE = {}


def kernel(**inputs):
    inp = {k: np.asarray(v) for k, v in inputs.items()}
    c = host_prep(inp)
    if "nc" not in _NC_CACHE:
        _NC_CACHE["nc"] = build()
    nc = _NC_CACHE["nc"]
    in_maps = []
    for b in range(8):
        m = dict(c)
        m["x"] = np.ascontiguousarray(inp["x"][b])
        in_maps.append(m)
    res = run_bass_kernel_spmd(nc, in_maps, core_ids=list(range(8)))
    return np.stack([r["out"] for r in res.results], axis=0).astype(np.float32)
```
